# Optimizing a Trainium2 kernel written in Bass

```python
import jax, jax.numpy as jnp
from jax import lax
import numpy as np

D_MODEL = 1024
BATCH = 1
SEQ = 16384
DEPTH = 2
DEC_BATCH = 32
DEC_SEQ = 1
PAST_LEN = 16384
PAGE_SIZE = 128

MIX_WIDTH = D_MODEL
HEAD_DIM = 64
A_WIDTH = MIX_WIDTH // 4
A_GROUPS = A_WIDTH // HEAD_DIM
B_WIDTH = MIX_WIDTH - A_WIDTH
B_HEADS = B_WIDTH // HEAD_DIM
CHUNK = 128
PATTERNS = ((128, 1), (512, 4), (2048, 16))
WINDOW = 2048
Q_BLOCK = 128
_SIZES = (A_WIDTH, A_WIDTH, A_WIDTH, B_WIDTH, B_WIDTH, B_WIDTH, B_WIDTH)
PROJ_WIDTH = sum(_SIZES)
SPLIT_AT = tuple(sum(_SIZES[:i + 1]) for i in range(len(_SIZES) - 1))
EPS = 1e-6

kernel_name = "hymba_sgu_dilated_swa_decoder_step"


def rms_norm(x, g):
    xf = x.astype(jnp.float32)
    y = xf * lax.rsqrt(jnp.mean(xf * xf, axis=-1, keepdims=True) + EPS)
    return (y * g.astype(jnp.float32)).astype(x.dtype)


def alibi_slopes():
    return 2.0 ** (-8.0 * jnp.arange(1, B_HEADS + 1, dtype=jnp.float32) / B_HEADS)


def chunk_spatial_gate(u, v, sgu_g, w_s, b_s):
    bn, t, _ = v.shape
    n_chunks = -(-t // CHUNK)
    pad = n_chunks * CHUNK - t
    vn = rms_norm(v, sgu_g)
    vp = jnp.pad(vn, ((0, 0), (0, pad), (0, 0))).reshape(bn, n_chunks, CHUNK, A_GROUPS, HEAD_DIM)
    mask = jnp.tril(jnp.ones((CHUNK, CHUNK), dtype=bool))
    w = jnp.where(mask[None], w_s, jnp.zeros_like(w_s))
    mixed = jnp.einsum('gts,bcsgd->bctgd', w, vp) + b_s.T[None, None, :, :, None]
    mixed = mixed.reshape(bn, n_chunks * CHUNK, A_WIDTH)[:, :t]
    return u * mixed, vn


def dilated_attend(q, k, v, q_idx, slopes):
    outs, lses = [], []
    for win, dil in PATTERNS:
        j = jnp.arange(win // dil + 1)
        idx = q_idx[:, None] - j[None, :] * dil
        valid = idx >= 0
        idxc = jnp.maximum(idx, 0)
        kg = k[:, idxc]
        vg = v[:, idxc]
        s = jnp.einsum('bqhd,bqjhd->bqhj', q, kg, preferred_element_type=jnp.float32)
        dist = (j * dil).astype(jnp.float32)
        s = s - slopes[:, None] * dist[None, :]
        s = jnp.where(valid[None, :, None, :], s, -jnp.inf)
        m = jnp.max(s, axis=-1, keepdims=True)
        p = jnp.exp(s - m)
        den = jnp.sum(p, axis=-1, keepdims=True)
        o = jnp.einsum('bqhj,bqjhd->bqhd', (p / den).astype(v.dtype), vg,
                       preferred_element_type=jnp.float32)
        outs.append(o)
        lses.append((m + jnp.log(den))[..., 0])
    wts = jax.nn.softmax(jnp.stack(lses, axis=0), axis=0)
    out = jnp.einsum('pbqh,pbqhd->bqhd', wts, jnp.stack(outs, axis=0))
    return out.astype(q.dtype)


def mixer_inputs(x, norm_g, w_in, q_g, k_g):
    bn, t, _ = x.shape
    h = rms_norm(x, norm_g)
    p = jnp.einsum('btd,de->bte', h, w_in)
    ua, va, za, q, k, v, zb = jnp.split(p, SPLIT_AT, axis=-1)
    q = rms_norm(q.reshape(bn, t, B_HEADS, HEAD_DIM), q_g) * (HEAD_DIM ** -0.5)
    k = rms_norm(k.reshape(bn, t, B_HEADS, HEAD_DIM), k_g)
    v = v.reshape(bn, t, B_HEADS, HEAD_DIM)
    return ua, va, za, q, k, v, zb


def mixer_output(x, a, za, o, zb, w_out):
    bn, t, _ = x.shape
    a = a * jax.nn.silu(za)
    b = o.reshape(bn, t, B_WIDTH) * jax.nn.silu(zb)
    return x + jnp.einsum('bte,ed->btd', jnp.concatenate([a, b], axis=-1), w_out)


def setup_inputs(seed: int = 0) -> dict:
    key = jax.random.key(seed)
    ks = jax.random.split(key, 12)
    cache_len = min(WINDOW, PAST_LEN)
    f32 = jnp.float32
    return {
        "x_prompt": jax.random.normal(ks[0], (BATCH, SEQ, D_MODEL), f32),
        "x_sample": jax.random.normal(ks[1], (DEC_BATCH, DEC_SEQ, D_MODEL), f32),
        "cache_k": jax.random.normal(ks[2], (DEPTH, DEC_BATCH, cache_len, B_HEADS, HEAD_DIM), f32),
        "cache_v": jax.random.normal(ks[3], (DEPTH, DEC_BATCH, cache_len, B_HEADS, HEAD_DIM), f32),
        "norm_g": 1.0 + 0.02 * jax.random.normal(ks[4], (DEPTH, D_MODEL), f32),
        "w_in": jax.random.normal(ks[5], (DEPTH, D_MODEL, PROJ_WIDTH), f32) * D_MODEL ** -0.5,
        "sgu_g": 1.0 + 0.02 * jax.random.normal(ks[6], (DEPTH, A_WIDTH), f32),
        "w_spatial": jax.random.normal(ks[7], (DEPTH, A_GROUPS, CHUNK, CHUNK), f32) * CHUNK ** -0.5,
        "b_spatial": 0.1 * jax.random.normal(ks[8], (DEPTH, A_GROUPS, CHUNK), f32),
        "q_norm_g": 1.0 + 0.02 * jax.random.normal(ks[9], (DEPTH, HEAD_DIM), f32),
        "k_norm_g": 1.0 + 0.02 * jax.random.normal(ks[10], (DEPTH, HEAD_DIM), f32),
        "w_out": jax.random.normal(ks[11], (DEPTH, MIX_WIDTH, D_MODEL), f32) * MIX_WIDTH ** -0.5,
    }


def reference(x_prompt, x_sample, cache_k, cache_v, norm_g, w_in, sgu_g, w_spatial, b_spatial,
              q_norm_g, k_norm_g, w_out):
    slopes = alibi_slopes()
    xp, xs = x_prompt, x_sample
    bp, tp, _ = xp.shape
    bs, ts, _ = xs.shape
    n_blocks = tp // Q_BLOCK
    keep_p = min(WINDOW, tp)
    cache_len = cache_k.shape[2]
    kp_new, vp_new, ks_new, vs_new, sgu_new = [], [], [], [], []
    for l in range(DEPTH):
        ua, va, za, q, k, v, zb = mixer_inputs(xp, norm_g[l], w_in[l], q_norm_g[l], k_norm_g[l])
        a_out, _ = chunk_spatial_gate(ua, va, sgu_g[l], w_spatial[l], b_spatial[l])
        qb = q.reshape(bp, n_blocks, Q_BLOCK, B_HEADS, HEAD_DIM).transpose(1, 0, 2, 3, 4)

        def attend_block(args, k=k, v=v):
            q_blk, i = args
            q_idx = i * Q_BLOCK + jnp.arange(Q_BLOCK)
            return dilated_attend(q_blk, k, v, q_idx, slopes)

        ob = lax.map(attend_block, (qb, jnp.arange(n_blocks)))
        o = ob.transpose(1, 0, 2, 3, 4).reshape(bp, tp, B_HEADS, HEAD_DIM)
        kp_new.append(k[:, tp - keep_p:])
        vp_new.append(v[:, tp - keep_p:])
        xp = mixer_output(xp, a_out, za, o, zb, w_out[l])

        ua, va, za, q, k, v, zb = mixer_inputs(xs, norm_g[l], w_in[l], q_norm_g[l], k_norm_g[l])
        a_out, vn = chunk_spatial_gate(ua, va, sgu_g[l], w_spatial[l], b_spatial[l])
        k_ext = jnp.concatenate([cache_k[l].astype(k.dtype), k], axis=1)
        v_ext = jnp.concatenate([cache_v[l].astype(v.dtype), v], axis=1)
        o = dilated_attend(q, k_ext, v_ext, cache_len + jnp.arange(ts), slopes)
        ks_new.append(k)
        vs_new.append(v)
        sgu_new.append(vn)
        xs = mixer_output(xs, a_out, za, o, zb, w_out[l])

    new_k_prompt = jnp.stack(kp_new, axis=0)
    new_v_prompt = jnp.stack(vp_new, axis=0)
    new_k_sample = jnp.stack(ks_new, axis=0)
    new_v_sample = jnp.stack(vs_new, axis=0)
    new_sgu_v_sample = jnp.stack(sgu_new, axis=0)
    return (xp, xs, new_k_prompt, new_v_prompt, new_k_sample, new_v_sample, new_sgu_v_sample)
```

```python
import numpy as np
import ml_dtypes
from contextlib import ExitStack
import concourse.bass as bass
import concourse.mybir as mybir
from concourse.bass_utils import run_bass_kernel_spmd

F32 = mybir.dt.float32
BF16 = mybir.dt.bfloat16
AF = mybir.ActivationFunctionType
ALU = mybir.AluOpType
AX = mybir.AxisListType
NPBF = ml_dtypes.bfloat16

NCORES = 8
D = 1024
PROJ = 3840
H = 12
DH = 64
ST = 2048
NT = 16
WIN = 3 * ST
EPS = 1e-6
NSEQ = 4
CL = 2048
NEG = -30000.0
import os
SKIP = set(os.environ.get('KSKIP', '').split(','))
PAIRW = 160

C_UA, C_VA, C_ZA, C_Q, C_K, C_V, C_ZB = 0, 256, 512, 768, 1536, 2304, 3072


def _slot_of_tok(w):
    return w


TOK = np.arange(WIN)
SLOT = _slot_of_tok(TOK)
TOK_OF_SLOT = np.empty(WIN, np.int64)
TOK_OF_SLOT[SLOT] = TOK
I_OF_P = np.arange(128)


def _split3(x):
    x = x.astype(np.float32)
    hi = x.astype(NPBF)
    r = x - hi.astype(np.float32)
    mid = r.astype(NPBF)
    r2 = r - mid.astype(np.float32)
    lo = r2.astype(NPBF)
    return hi, mid, lo


def _slopes():
    return (2.0 ** (-8.0 * np.arange(1, H + 1, dtype=np.float32) / H)).astype(np.float32)


def _pattern_index():
    e = np.arange(128)
    return [e.copy(), e.copy(), e.copy()]


class Em:
    def __init__(self, nc, ndma=24):
        self.nc = nc
        self.engs = ['pe', 'act', 'dve', 'pool', 'sp']
        self.ops = {e: [] for e in self.engs}
        self.cnt = {e: 0 for e in self.engs}
        self.ndma = ndma
        self.dcnt = [0] * ndma
        self.dnext = 0
        self.lastw = {}
        self.readers = {}
        self.waited = {e: {} for e in self.engs}
        self.pending = {e: {} for e in self.engs}
        self.alias = {}

    def _exp(self, keys):
        out = []
        for k in keys:
            out.extend(self.alias.get(k, [k]))
        return out

    def _excl(self, reads, writes):
        reads = self._exp(reads)
        writes = self._exp(writes)
        ps = [k for k in reads if isinstance(k, str) and k[0] == 'B' and len(k) <= 3 and k[1].isdigit()]
        if ps:
            reads = [k for k in reads if k not in ps]
            writes = list(writes) + ps
        return reads, writes

    def barrier(self):
        for e in self.engs:
            for o in ['pe', 'act', 'dve', 'pool']:
                if o != e and self.cnt[o] > 0:
                    self.pending[e][o] = self.cnt[o]
            for i in range(self.ndma):
                if self.dcnt[i] > 0:
                    self.pending[e][('d', i)] = 16 * self.dcnt[i]

    def _deps(self, eng, reads, writes):
        reads, writes = self._excl(reads, writes)
        toks = list(self.pending[eng].items())
        self.pending[eng] = {}
        for r in reads:
            if r in self.lastw:
                toks.append(self.lastw[r])
        for w in writes:
            if w in self.lastw:
                toks.append(self.lastw[w])
            toks.extend(self.readers.get(w, []))
        waits = {}
        for (k, v) in toks:
            if k == 'pe' and eng == 'pe':
                continue
            if self.waited[eng].get(k, 0) >= v:
                continue
            waits[k] = max(waits.get(k, 0), v)
        for k, v in waits.items():
            self.waited[eng][k] = v
        return waits

    def _commit(self, tok, reads, writes):
        reads, writes = self._excl(reads, writes)
        for r in reads:
            self.readers.setdefault(r, []).append(tok)
        for w in writes:
            self.lastw[w] = tok
            self.readers[w] = []

    def op(self, eng, fn, reads=(), writes=()):
        waits = self._deps(eng, reads, writes)
        self.cnt[eng] += 1
        tok = (eng, self.cnt[eng])
        self.ops[eng].append((fn, list(waits.items()), (eng, 1)))
        self._commit(tok, reads, writes)

    def dma(self, fn, reads=(), writes=(), q='sp'):
        i = self.dnext
        self.dnext = (self.dnext + 1) % self.ndma
        waits = self._deps(q, reads, writes)
        k = ('d', i)
        if self.dcnt[i] > 0 and self.waited[q].get(k, 0) < 16 * self.dcnt[i]:
            waits[k] = 16 * self.dcnt[i]
            self.waited[q][k] = 16 * self.dcnt[i]
        self.dcnt[i] += 1
        tok = (k, 16 * self.dcnt[i])
        self.ops[q].append((fn, list(waits.items()), (k, 16)))
        self._commit(tok, reads, writes)

    def emit(self):
        nc = self.nc
        with ExitStack() as st:
            sem = {}
            for e in ['pe', 'act', 'dve', 'pool']:
                sem[e] = st.enter_context(nc.semaphore(f"tl_{e}"))
            for i in range(self.ndma):
                sem[('d', i)] = st.enter_context(nc.semaphore(f"dq{i}"))
            block = st.enter_context(nc.Block())
            final = [(('d', i), 16 * self.dcnt[i]) for i in range(self.ndma) if self.dcnt[i] > 0]

            def run(engh, lst, is_sp=False):
                for fn, waits, (k, amt) in lst:
                    for (wk, wv) in waits:
                        engh.wait_ge(sem[wk], wv)
                    ins = fn(engh)
                    ins.then_inc(sem[k], amt)
                if is_sp:
                    for (wk, wv) in final:
                        engh.wait_ge(sem[wk], wv)
                    for e in ['pe', 'act', 'dve', 'pool']:
                        if self.cnt[e] > 0:
                            engh.wait_ge(sem[e], self.cnt[e])

            @block.sync
            def _(e):
                run(e, self.ops['sp'], True)

            @block.tensor
            def _(e):
                run(e, self.ops['pe'])

            @block.scalar
            def _(e):
                run(e, self.ops['act'])

            @block.vector
            def _(e):
                run(e, self.ops['dve'])

            @block.gpsimd
            def _(e):
                run(e, self.ops['pool'])


def build_nc(do_prompt=True, do_sample=True, nlayers=2, plim=None):
    nc = bass.Bass("TRN2", target_bir_lowering=False)

    def din(name, shape, dt=F32):
        return nc.dram_tensor(name, list(shape), dt, kind="ExternalInput").ap()

    def dout(name, shape, dt=F32):
        return nc.dram_tensor(name, list(shape), dt, kind="ExternalOutput").ap()

    def dscr(name, shape, dt):
        return nc.dram_tensor(name, list(shape), dt, kind="Internal").ap()

    xw = din("xw", [WIN, D])
    w_in = din("w_in", [2, D, PROJ])
    w_out = din("w_out", [2, D, D])
    norm_g = din("norm_g", [2, D])
    sgu_g = din("sgu_g", [2, 256])
    qg = din("qg", [2, DH])
    kg = din("kg", [2, DH])
    wsp = din("wsp", [2, 128, 4, 128])
    wmask = din("wmask", [128, 128])
    bsp = din("bsp", [2, 128, 4])
    augk = din("augk", [H, 7, WIN], BF16)
    augq = din("augq", [H, 7, WIN], BF16)
    maskb = din("maskb", [128, 3, 256], BF16)
    ident_d = din("ident", [128, 128], BF16)
    sel_d = din("sel", [128, 2, 128], BF16)
    xs = din("xs", [NSEQ, D])
    ck = din("ck", [2, NSEQ, CL, 768])
    cv = din("cv", [2, NSEQ, CL, 768])
    sbias = din("sbias", [128, 3, H])
    bdm = din("bdm", [H, 768])
    selq_d = din("selq", [NSEQ, NSEQ, 128], BF16)
    selo_d = din("selo", [H, NSEQ, NSEQ], BF16)
    w00 = din("w00", [2, 256])
    b00 = din("b00", [2, 256])

    y_o = dout("y", [ST, D])
    nk_o = dout("nk", [2, ST, 768])
    nv_o = dout("nv", [2, ST, 768])
    ys_o = dout("ys", [NSEQ, D])
    nks_o = dout("nks", [2, NSEQ, 768])
    nvs_o = dout("nvs", [2, NSEQ, 768])
    nsg_o = dout("nsg", [2, NSEQ, 256])

    QT = dscr("QT", [DH, H, ST], BF16)
    KT = dscr("KT", [DH, H, WIN], BF16)
    VD = [dscr(f"VD{p}", [WIN, 6 * PAIRW], BF16) for p in range(3)]
    SZ = dscr("SZ", [128, 6, ST], BF16)
    X1 = dscr("X1", [WIN, D], F32)
    XSD = dscr("XSD", [NSEQ, D], F32)

    em = Em(nc)
    em.alias.update({
        'PC0': ['B0'], 'PC1': ['B1'],
        'PSS0': ['B0'], 'PSS1': ['B1'], 'PSS2': ['B2'], 'PSS3': ['B7'],
        'PSO0': ['B3'], 'PSO1': ['B5'], 'PSO2': ['B6'],
        'PS2': ['B2'],
        'PS3a': ['B3'], 'PS3s': ['B3'],
        'PS4a': ['B4'], 'PS4k': ['B4'], 'PS4n': ['B4'], 'PS4s': ['B4'],
        'PS5a': ['B5'], 'PS6b': ['B6'], 'PS6o': ['B6'],
        'PS7a': ['B7'], 'PS7b': ['B7'], 'PS7o': ['B7'],
        'Win': [f'Win{kc}_{ci}' for kc in range(8) for ci in range(3)],
        'Wout': [f'Wout{kc}' for kc in range(8)],
    })
    slopes = _slopes()

    with ExitStack() as st:
        ARENA_ELEMS = 103 * 1024 + 256
        arena = st.enter_context(nc.sbuf_tensor("arena", [128, ARENA_ELEMS], BF16))
        off = [0]

        def alloc(nbytes):
            n = (nbytes + 63) // 64 * 32
            o = off[0]
            off[0] += n
            assert off[0] <= ARENA_ELEMS, f"SBUF arena overflow {off[0]*2}"
            return o

        def vb(o, n):
            return arena[:, o:o + n]

        def vf(o, n):
            return arena[:, o:o + 2 * n].bitcast(F32)

        psum = [st.enter_context(nc.psum_tensor(f"ps{i}", [128, 512], F32)) for i in range(8)]

        def pf(i):
            return psum[i][:, :]

        def pb(i):
            return psum[i][:, :].bitcast(BF16)

        o_win = alloc(8 * PROJ * 2)
        Win = vb(o_win, 8 * PROJ).rearrange("p (k c) -> p k c", k=8)
        o_wout = alloc(8 * D * 2)
        Wout = vb(o_wout, 8 * D).rearrange("p (k c) -> p k c", k=8)
        o_gb = alloc(D * 4)
        GB = vf(o_gb, D)
        o_c = alloc(64 * 4); GQ = vf(o_c, 64)
        o_c = alloc(64 * 4); GK = vf(o_c, 64)
        o_c = alloc(256 * 4); SGB = vf(o_c, 256)
        o_c = alloc(128 * 2); IDN = vb(o_c, 128)
        o_c = alloc(256 * 2); SEL = vb(o_c, 256).rearrange("p (a b) -> p a b", a=2)
        o_c = alloc(768 * 2); MB = vb(o_c, 768).rearrange("p (a b) -> p a b", a=3)
        o_c = alloc(512 * 4); WSPF = vf(o_c, 512)
        o_c = alloc(128 * 4); WMF = vf(o_c, 128)
        o_c = alloc(512 * 2); WSPB = vb(o_c, 512).rearrange("p (g t) -> p g t", g=4)
        o_c = alloc(4 * 4); BSP = vf(o_c, 4)
        o_c = alloc(4 * 4); EPSC = vf(o_c, 4)
        o_at = alloc(2 * ST * 2)
        AT = vb(o_at, 2 * ST).rearrange("p (c s) -> p c s", c=2)
        o_gt = alloc(6 * ST * 2)
        GT = vb(o_gt, 6 * ST).rearrange("p (c s) -> p c s", c=6)
        o_xt = alloc(D * 4); XT = vf(o_xt, D)
        o_xt1 = alloc(D * 4); XT1 = vf(o_xt1, D)
        o_ysb = alloc(D * 4); YSB = vf(o_ysb, D)
        o_va = alloc(6 * PAIRW * 2); VAUG = vb(o_va, 6 * PAIRW).rearrange("p (c w) -> p c w", c=6)
        u0 = off[0]
        o_junk = alloc(D * 2); JUNK = vb(o_junk, D)
        o_h = alloc(D * 2); HB = vb(o_h, D)
        o_uv = alloc(512 * 4); UV = vf(o_uv, 512)
        o_sza = alloc(256 * 4); SZA = vf(o_sza, 256)
        o_qk = alloc(1536 * 4); QK = vf(o_qk, 1536)
        o_sq = alloc(1536 * 4); SQ = vf(o_sq, 1536)
        o_qa = alloc(768 * 2); QA = vb(o_qa, 768)
        o_ka = alloc(768 * 2); KA = vb(o_ka, 768)
        o_k32 = alloc(768 * 4); K32 = vf(o_k32, 768)
        o_v32 = alloc(768 * 4); V32 = vf(o_v32, 768)
        o_szb = alloc(768 * 2); SZB = vb(o_szb, 768)
        o_vn = alloc(256 * 4); VN32 = vf(o_vn, 256)
        o_au = alloc(256 * 4); AU = vf(o_au, 256)
        o_st = alloc(64 * 4); STAT = vf(o_st, 64)
        u1 = off[0]
        o_ = alloc(D * 2); JUNK1 = vb(o_, D)
        o_ = alloc(D * 2); HB1 = vb(o_, D)
        o_ = alloc(512 * 4); UV1 = vf(o_, 512)
        o_ = alloc(256 * 4); SZA1 = vf(o_, 256)
        o_ = alloc(1536 * 4); QK1 = vf(o_, 1536)
        o_ = alloc(768 * 4); V321 = vf(o_, 768)
        o_ = alloc(768 * 2); SZB1 = vb(o_, 768)
        o_ = alloc(64 * 4); STAT1 = vf(o_, 64)
        o_ = alloc(D * 2); HT1 = vb(o_, D).rearrange("p (k t) -> p k t", k=8)
        o_ht = alloc(D * 2); HT = vb(o_ht, D).rearrange("p (k t) -> p k t", k=8)
        o_vnb = alloc(256 * 2); VNB = vb(o_vnb, 256)
        o_ag = alloc(256 * 2); AGB = vb(o_ag, 256)
        QTB = 2
        o_qst = alloc(H * 128 * QTB * 2); QST = vb(o_qst, H * 128 * QTB).rearrange("p (h s) -> p h s", h=H)
        o_kst = alloc(H * 128 * QTB * 2); KST = vb(o_kst, H * 128 * QTB).rearrange("p (h s) -> p h s", h=H)
        o_zst = alloc(6 * 128 * QTB * 2); ZST = vb(o_zst, 6 * 128 * QTB).rearrange("p (h s) -> p h s", h=6)
        endA = off[0]
        off[0] = u0
        QTHS, KTHS = [], []
        for i in range(2):
            o_qth = alloc(ST * 2); QTHS.append(vb(o_qth, ST))
            o_kth = alloc(2 * ST * 2); KTHS.append(vb(o_kth, 2 * ST))
        VB = []
        for p in range(2):
            o_v = alloc(32 * PAIRW * 2)
            VB.append(vb(o_v, 32 * PAIRW).rearrange("p (b w) -> p b w", b=32))
        NPT = 4
        PT = []
        for i in range(NPT):
            o_p = alloc(256 * 2)
            PT.append(vb(o_p, 256))
        OTMP = []
        for i in range(2):
            o_ = alloc(128 * 4); OTMP.append(vf(o_, 128))
        o_oe = alloc(ST * 4); OE = vf(o_oe, ST)
        o_oo = alloc(ST * 4); OO = vf(o_oo, ST)
        o_szp = alloc(ST * 2); SZP = vb(o_szp, ST)
        o_rd = alloc(512 * 4); RD = vf(o_rd, 512)
        o_rh = alloc(512 * 2); RH = vb(o_rh, 512)
        o_rl = alloc(512 * 2); RL = vb(o_rl, 512)
        o_tmp = alloc(512 * 4); TMPO = vf(o_tmp, 512)
        endB = off[0]
        off[0] = max(endA, endB)
        print("[kernel] SBUF bytes/partition: shared", u0 * 2, "endA", endA * 2, "endB", endB * 2)

        em.dma(lambda e: e.dma_start(out=IDN, in_=ident_d), writes=['IDN'])
        em.dma(lambda e: e.dma_start(out=SEL, in_=sel_d), writes=['SEL'])
        em.dma(lambda e: e.dma_start(out=MB, in_=maskb), writes=['MB'])
        em.dma(lambda e: e.dma_start(out=WMF, in_=wmask), writes=['WMF'])

        def load_layer_consts(l):
            em.dma(lambda e: e.dma_start(out=GB, in_=norm_g[l:l + 1, :].broadcast_to([128, D])), writes=['GB'])
            em.dma(lambda e: e.dma_start(out=GQ, in_=qg[l:l + 1, :].broadcast_to([128, DH])), writes=['GQ'])
            em.dma(lambda e: e.dma_start(out=GK, in_=kg[l:l + 1, :].broadcast_to([128, DH])), writes=['GK'])
            em.dma(lambda e: e.dma_start(out=SGB, in_=sgu_g[l:l + 1, :].broadcast_to([128, 256])), writes=['SGB'])
            em.dma(lambda e: e.dma_start(out=WSPF, in_=wsp[l].rearrange("p g t -> p (g t)")), writes=['WSPF'])
            em.dma(lambda e: e.dma_start(out=BSP, in_=bsp[l]), writes=['BSP'])
            em.op('pool', lambda e: e.tensor_tensor(
                out=WSPB, in0=WSPF.rearrange("p (g t) -> p g t", g=4),
                in1=WMF.unsqueeze(1).broadcast_to([128, 4, 128]), op=ALU.mult),
                reads=['WSPF', 'WMF'], writes=['WSPB'])
            stg = [(QK, 'QK'), (QK1, 'QK1'), (SQ, 'SQ')]
            engs = ['pool', 'act', 'dve']
            n = 0
            for kc in range(8):
                for ci, c0 in enumerate(range(0, PROJ, 1536)):
                    cw = min(1536, PROJ - c0)
                    sb, sk = stg[n % 3]
                    eng = engs[n % 3]
                    n += 1
                    em.dma(lambda e, sb=sb, kc=kc, c0=c0, cw=cw: e.dma_start(
                        out=sb[:, 0:cw], in_=w_in[l, kc * 128:(kc + 1) * 128, c0:c0 + cw]), writes=[sk])
                    if eng == 'act':
                        em.op('act', lambda e, sb=sb, kc=kc, c0=c0, cw=cw: e.activation(
                            out=Win[:, kc, c0:c0 + cw], in_=sb[:, 0:cw], func=AF.Copy), reads=[sk], writes=[f'Win{kc}_{ci}'])
                    else:
                        em.op(eng, lambda e, sb=sb, kc=kc, c0=c0, cw=cw: e.tensor_copy(
                            out=Win[:, kc, c0:c0 + cw], in_=sb[:, 0:cw]), reads=[sk], writes=[f'Win{kc}_{ci}'])
            for kc in range(8):
                sb, sk = stg[n % 3]
                eng = engs[n % 3]
                n += 1
                em.dma(lambda e, sb=sb, kc=kc: e.dma_start(
                    out=sb[:, 0:D], in_=w_out[l, kc * 128:(kc + 1) * 128, :]), writes=[sk])
                if eng == 'act':
                    em.op('act', lambda e, sb=sb, kc=kc: e.activation(
                        out=Wout[:, kc, :], in_=sb[:, 0:D], func=AF.Copy), reads=[sk], writes=[f'Wout{kc}'])
                else:
                    em.op(eng, lambda e, sb=sb, kc=kc: e.tensor_copy(
                        out=Wout[:, kc, :], in_=sb[:, 0:D]), reads=[sk], writes=[f'Wout{kc}'])

        em.op('pool', lambda e: e.memset(EPSC[:, 0:1], EPS), writes=['EPSC'])
        em.op('pool', lambda e: e.memset(EPSC[:, 1:2], DH * EPS), writes=['EPSC'])
        em.op('pool', lambda e: e.memset(EPSC[:, 2:3], 1e-30), writes=['EPSC'])
        em.op('pool', lambda e: e.memset(VAUG, 0.0), writes=['VAUG'])
        em.op('pool', lambda e: e.memset(VAUG[:, :, 64:65], 1.0), writes=['VAUG'])

        class _BS:
            pass
        S0, S1 = _BS(), _BS()
        for S_, sfx, bufs in ((S0, '', (XT, HB, JUNK, UV, SZA, QK, V32, SZB, STAT, HT)),
                              (S1, '1', (XT1, HB1, JUNK1, UV1, SZA1, QK1, V321, SZB1, STAT1, HT1))):
            (S_.XT, S_.HB, S_.JUNK, S_.UV, S_.SZA, S_.QK, S_.V32, S_.SZB, S_.STAT, S_.HT) = bufs
            for nm in ('XT', 'HB', 'JUNK', 'UV', 'SZA', 'QK', 'V32', 'SZB', 'ST', 'HT', 'SSQ', 'PW'):
                setattr(S_, 'k' + nm, nm + sfx)
        BS = [S0, S1]

        def rmsnorm_rows(np_, src, src_key, width, gain, gain_key, out, out_key, stat_col, S=None):
            S = S or S0
            ss = S.STAT[0:np_, stat_col:stat_col + 1]
            rs = S.STAT[0:np_, stat_col + 1:stat_col + 2]
            k0, k1 = f'{S.kST}{stat_col}', f'{S.kST}{stat_col + 1}'
            em.op('act', lambda e: e.activation(out=S.JUNK[0:np_, 0:width], in_=src, func=AF.Square, accum_out=ss),
                  reads=[src_key], writes=[S.kJUNK, k0])
            em.op('act', lambda e: e.activation(out=rs, in_=ss, func=AF.Sqrt, bias=EPSC[0:np_, 0:1], scale=1.0 / width),
                  reads=[k0, 'EPSC'], writes=[k1])
            em.op('dve', lambda e: e.reciprocal(out=rs, in_=rs), reads=[k1], writes=[k1])
            em.op('dve', lambda e: e.scalar_tensor_tensor(out=out, in0=src, scalar=rs, in1=gain,
                                                          op0=ALU.mult, op1=ALU.mult),
                  reads=[src_key, k1, gain_key], writes=[out_key])

        def qk_norm(np_, S=None, konly=False, part=None):
            S = S or S0
            c0 = 768 if konly else 0
            h0 = 12 if konly else 0
            qk = S.QK[0:np_, c0:1536]
            sq = SQ[0:np_, c0:1536]
            ssq = S.STAT[0:np_, 8 + h0:32]
            pw = S.STAT[0:np_, 32 + h0:56]
            nh = 24 - h0
            if part in (None, 'a'):
                em.op('pool', lambda e: e.tensor_tensor(out=sq, in0=qk, in1=qk, op=ALU.mult), reads=[S.kQK], writes=['SQ'])
            if part == 'a':
                return
            if part in (None, 'b', 'b1'):
                em.op('dve', lambda e: e.tensor_reduce(out=ssq, in_=sq.rearrange("p (h d) -> p h d", d=DH),
                                                       axis=AX.X, op=ALU.add), reads=['SQ'], writes=[S.kSSQ])
            if part == 'b1':
                return
            em.op('act', lambda e: e.activation(out=pw, in_=ssq, func=AF.Sqrt, bias=EPSC[0:np_, 1:2], scale=1.0),
                  reads=[S.kSSQ, 'EPSC'], writes=[S.kPW])
            em.op('dve', lambda e: e.reciprocal(out=pw, in_=pw), reads=[S.kPW], writes=[S.kPW])
            pwk = S.STAT[0:np_, 44:56]
            em.op('dve', lambda e: e.tensor_scalar(out=pwk, in0=pwk, scalar1=8.0, scalar2=None,
                                                   op0=ALU.mult), reads=[S.kPW], writes=[S.kPW])
            em.op('dve', lambda e: e.tensor_tensor(
                out=sq.rearrange("p (h d) -> p h d", d=DH), in0=qk.rearrange("p (h d) -> p h d", d=DH),
                in1=pw.unsqueeze(2).broadcast_to([np_, nh, DH]), op=ALU.mult), reads=[S.kQK, S.kPW], writes=['SQ'])
            if not konly:
                em.op('pool', lambda e: e.tensor_tensor(
                    out=QA[0:np_, :].rearrange("p (h d) -> p h d", d=DH),
                    in0=SQ[0:np_, 0:768].rearrange("p (h d) -> p h d", d=DH),
                    in1=GQ[0:np_, :].unsqueeze(1).broadcast_to([np_, H, DH]), op=ALU.mult),
                    reads=['SQ', 'GQ'], writes=['QA'])
            em.op('dve', lambda e: e.tensor_tensor(
                out=K32[0:np_, :].rearrange("p (h d) -> p h d", d=DH),
                in0=SQ[0:np_, 768:1536].rearrange("p (h d) -> p h d", d=DH),
                in1=GK[0:np_, :].unsqueeze(1).broadcast_to([np_, H, DH]), op=ALU.mult),
                reads=['SQ', 'GK'], writes=['K32'])
            em.op('pool', lambda e: e.tensor_copy(out=KA[0:np_, :], in_=K32[0:np_, :]),
                  reads=['K32'], writes=['KA'])

        CH_FULL = [(0, 512), (512, 256), (768, 512), (1280, 256), (1536, 512), (2048, 256),
                   (2304, 512), (2816, 256), (3072, 512), (3584, 256)]
        CH_KV = [(1536, 512), (2048, 256), (2304, 512), (2816, 256)]

        def inproj_and_evac(np_, ht_fn, ht_key, full, pcn, S=None, hook=None, hook_after=None, hook2=None, hook2_after=None):
            S = S or S0
            for ci_, (c0, cw) in enumerate(CH_FULL if full else CH_KV):
                ring_b = (0, 1, 4) if full else (0, 1, 4, 6)
                ring_k = ('PC0', 'PC1', 'PS4a', 'PS6b')
                bi_ = pcn[0] % len(ring_b)
                pcn[0] += 1
                bank = ring_b[bi_]
                pk = ring_k[bi_]
                ps = pf(bank)[0:np_, 0:cw]
                for kc in range(8):
                    em.op('pe', lambda e, ps=ps, kc=kc, c0=c0, cw=cw: e.matmul(
                        ps, lhsT=ht_fn(kc), rhs=Win[:, kc, c0:c0 + cw], start=(kc == 0), stop=(kc == 7)),
                        reads=[ht_key, 'Win'], writes=[pk])
                if c0 == 0:
                    em.op('act', lambda e, ps=ps: e.activation(out=S.UV[0:np_, :], in_=ps, func=AF.Copy),
                          reads=[pk], writes=[S.kUV])
                elif c0 == 512:
                    em.op('act', lambda e, ps=ps: e.activation(out=S.SZA[0:np_, :], in_=ps, func=AF.Silu),
                          reads=[pk], writes=[S.kSZA])
                elif c0 in (768, 1280, 1536, 2048):
                    d0 = c0 - 768
                    if full:
                        em.op('dve', lambda e, ps=ps, d0=d0, cw=cw: e.tensor_copy(out=S.QK[0:np_, d0:d0 + cw], in_=ps),
                              reads=[pk], writes=[S.kQK])
                    else:
                        em.op('act', lambda e, ps=ps, d0=d0, cw=cw: e.activation(
                            out=S.QK[0:np_, d0:d0 + cw], in_=ps, func=AF.Copy), reads=[pk], writes=[S.kQK])
                elif c0 in (2304, 2816):
                    d0 = c0 - 2304
                    em.op('act', lambda e, ps=ps, d0=d0, cw=cw: e.activation(
                        out=S.V32[0:np_, d0:d0 + cw], in_=ps, func=AF.Copy), reads=[pk], writes=[S.kV32])
                else:
                    d0 = c0 - 3072
                    em.op('act', lambda e, ps=ps, d0=d0, cw=cw: e.activation(
                        out=S.SZB[0:np_, d0:d0 + cw], in_=ps, func=AF.Silu), reads=[pk], writes=[S.kSZB])
                if hook is not None and ci_ == hook_after:
                    hook()
                if hook2 is not None and ci_ == hook2_after:
                    hook2()

        pcn = [0]

        def stage_a(l, s, full):
            own = (s == 2)
            srcx = xw if l == 0 else X1

            def L(j):
                S = BS[j % 2]
                gt = s * NT + j
                r0 = gt * 128
                em.dma(lambda e, r0=r0: e.dma_start(out=S.XT, in_=srcx[r0:r0 + 128, :]),
                       reads=([f'X1_{gt}'] if l else []), writes=[S.kXT])

            def F1(j):
                S = BS[j % 2]
                rmsnorm_rows(128, S.XT, S.kXT, D, GB, 'GB', S.HB, S.kHB, 0, S)

            def F2(j):
                S = BS[j % 2]
                for kc in range(8):
                    em.op('pe', lambda e, kc=kc: e.transpose(pb(2)[:, kc * 128:(kc + 1) * 128],
                                                             S.HB[:, kc * 128:(kc + 1) * 128], IDN),
                          reads=[S.kHB, 'IDN'], writes=['PS2'])
                em.op('act', lambda e: e.activation(out=S.HT.rearrange("p k t -> p (k t)"), in_=pb(2), func=AF.Copy),
                      reads=['PS2'], writes=[S.kHT])

            def M(j, hook=None):
                S = BS[j % 2]
                h2 = (lambda: (qk_norm(128, S, konly=(not full), part='a'), qk_norm(128, S, konly=(not full), part='b1')))
                inproj_and_evac(128, lambda kc: S.HT[:, kc, :], S.kHT, full, pcn, S,
                                hook=hook, hook_after=0, hook2=h2, hook2_after=(5 if full else 1))

            def B1b(j):
                qk_norm(128, BS[j % 2], konly=(not full), part='b2')

            def B1(j):
                S = BS[j % 2]
                if full:
                    rmsnorm_rows(128, S.UV[:, 256:512], S.kUV, 256, SGB, 'SGB', VNB, 'VNB', 2, S)
                srcv = S.V32.rearrange("p (c t d) -> p c t d", t=2, d=DH)
                em.op('pool', lambda e: e.tensor_copy(out=VAUG[:, :, 0:64], in_=srcv[:, :, 0, :]),
                      reads=[S.kV32], writes=['VAUG'])
                em.op('pool', lambda e: e.tensor_copy(out=VAUG[:, :, 96:160], in_=srcv[:, :, 1, :]),
                      reads=[S.kV32], writes=['VAUG'])

            def SG(j):
                S = BS[j % 2]
                for g in range(4):
                    em.op('pe', lambda e, g=g: e.matmul(pf(6)[:, g * 64:(g + 1) * 64],
                                                        lhsT=WSPB[:, g, :], rhs=VNB[:, g * 64:(g + 1) * 64],
                                                        start=True, stop=True),
                          reads=['WSPB', 'VNB'], writes=['PS6b'])
                for g in range(4):
                    em.op('dve', lambda e, g=g: e.scalar_tensor_tensor(
                        out=AU[:, g * 64:(g + 1) * 64], in0=pf(6)[:, g * 64:(g + 1) * 64],
                        scalar=BSP[:, g:g + 1], in1=S.UV[:, g * 64:(g + 1) * 64], op0=ALU.add, op1=ALU.mult),
                        reads=['PS6b', 'BSP', S.kUV], writes=['AU'])
                em.op('pool', lambda e: e.tensor_tensor(out=AGB, in0=AU, in1=S.SZA, op=ALU.mult),
                      reads=['AU', S.kSZA], writes=['AGB'])

            def B2(j):
                S = BS[j % 2]
                gt = s * NT + j
                r0 = gt * 128
                bq = j % QTB
                if full:
                    for h in range(8):
                        em.op('pe', lambda e, h=h: e.transpose(
                            pb(3)[0:64, h * 128:(h + 1) * 128], QA[:, h * 64:(h + 1) * 64], IDN),
                            reads=['QA', 'IDN'], writes=['PS3a'])
                    em.op('act', lambda e: e.activation(
                        out=QST[0:64, 0:8, bq * 128:(bq + 1) * 128],
                        in_=pb(3)[0:64, :].rearrange("p (h t) -> p h t", h=8), func=AF.Copy),
                        reads=['PS3a'], writes=['QST'])
                for h in range(8):
                    em.op('pe', lambda e, h=h: e.transpose(
                        pb(5)[0:64, h * 128:(h + 1) * 128], KA[:, h * 64:(h + 1) * 64], IDN),
                        reads=['KA', 'IDN'], writes=['PS5a'])
                em.op('dve', lambda e: e.tensor_copy(
                    out=KST[0:64, 0:8, bq * 128:(bq + 1) * 128],
                    in_=pb(5)[0:64, :].rearrange("p (h t) -> p h t", h=8)), reads=['PS5a'], writes=['KST'])
                if full:
                    for h in range(8, 12):
                        em.op('pe', lambda e, h=h: e.transpose(
                            pb(2)[0:64, (h - 8) * 128:(h - 7) * 128], QA[:, h * 64:(h + 1) * 64], IDN),
                            reads=['QA', 'IDN'], writes=['PS2'])
                for h in range(8, 12):
                    em.op('pe', lambda e, h=h: e.transpose(
                        pb(2)[0:64, (h - 4) * 128:(h - 3) * 128], KA[:, h * 64:(h + 1) * 64], IDN),
                        reads=['KA', 'IDN'], writes=['PS2'])
                if full:
                    em.op('act', lambda e: e.activation(
                        out=QST[0:64, 8:12, bq * 128:(bq + 1) * 128],
                        in_=pb(2)[0:64, 0:512].rearrange("p (h t) -> p h t", h=4), func=AF.Copy),
                        reads=['PS2'], writes=['QST'])
                em.op('dve', lambda e: e.tensor_copy(
                    out=KST[0:64, 8:12, bq * 128:(bq + 1) * 128],
                    in_=pb(2)[0:64, 512:1024].rearrange("p (h t) -> p h t", h=4)), reads=['PS2'], writes=['KST'])
                if bq == QTB - 1:
                    g0 = (gt - (QTB - 1)) * 128
                    l0 = (j - (QTB - 1)) * 128
                    if full:
                        em.dma(lambda e: e.dma_start(out=QT[:, :, l0:l0 + 128 * QTB], in_=QST[0:64, :, :]),
                               reads=['QST'], writes=['QTd'])
                    em.dma(lambda e: e.dma_start(out=KT[:, :, g0:g0 + 128 * QTB], in_=KST[0:64, :, :]),
                           reads=['KST'], writes=[f'KTd{s}'])
                em.dma(lambda e: e.dma_start(out=VD[0][r0:r0 + 128, :], in_=VAUG.rearrange("p c w -> p (c w)")),
                       reads=['VAUG'], writes=[f'VD0_{s}'])
                v2 = VD[1][s * ST:(s + 1) * ST, :].rearrange("(m r jj pp) w -> m jj pp r w", m=4, r=4, jj=4, pp=32)
                em.dma(lambda e: e.dma_start(out=v2[j // 4, j % 4], in_=VAUG.rearrange("p c w -> p (c w)")),
                       reads=['VAUG'], writes=[f'VD1_{s}'])
                v3 = VD[2][s * ST:(s + 1) * ST, :].rearrange("(r jj pp) w -> jj pp r w", r=16, jj=16, pp=8)
                em.dma(lambda e: e.dma_start(out=v3[j], in_=VAUG.rearrange("p c w -> p (c w)")),
                       reads=['VAUG'], writes=[f'VD2_{s}'])
                if own:
                    em.dma(lambda e: e.dma_start(out=nk_o[l, j * 128:(j + 1) * 128, :], in_=K32), reads=['K32'])
                    em.dma(lambda e: e.dma_start(out=nv_o[l, j * 128:(j + 1) * 128, :], in_=S.V32), reads=[S.kV32])
                if not full:
                    return
                for c in range(6):
                    em.op('pe', lambda e, c=c: e.transpose(pb(7)[:, c * 128:(c + 1) * 128],
                                                           S.SZB[:, c * 128:(c + 1) * 128], IDN),
                          reads=[S.kSZB, 'IDN'], writes=['PS7a'])
                for c in range(2):
                    em.op('pe', lambda e, c=c: e.transpose(pb(7)[:, 768 + c * 128:768 + (c + 1) * 128],
                                                           AGB[:, c * 128:(c + 1) * 128], IDN),
                          reads=['AGB', 'IDN'], writes=['PS7a'])
                em.op('act', lambda e: e.activation(
                    out=ZST[:, :, bq * 128:(bq + 1) * 128],
                    in_=pb(7)[:, 0:768].rearrange("p (c t) -> p c t", c=6), func=AF.Copy),
                    reads=['PS7a'], writes=['ZST'])
                em.op('act', lambda e: e.activation(
                    out=AT[:, :, j * 128:(j + 1) * 128],
                    in_=pb(7)[:, 768:1024].rearrange("p (c t) -> p c t", c=2), func=AF.Copy),
                    reads=['PS7a'], writes=['AT'])
                if bq == QTB - 1:
                    l0 = (j - (QTB - 1)) * 128
                    em.dma(lambda e: e.dma_start(out=SZ[:, :, l0:l0 + 128 * QTB], in_=ZST),
                           reads=['ZST'], writes=['SZd'])

            L(0)
            for t in range(NT + 2):
                if t + 1 < NT:
                    L(t + 1)
                if t < NT:
                    F1(t)
                if 0 <= t - 2 < NT:
                    B1(t - 2)
                hk = (lambda t=t: B1b(t - 2)) if 0 <= t - 2 < NT else None
                if 0 <= t - 1 < NT:
                    M(t - 1, hk)
                elif hk is not None:
                    hk()
                if full and 0 <= t - 2 < NT:
                    SG(t - 2)
                if t < NT:
                    F2(t)
                if 0 <= t - 2 < NT:
                    B2(t - 2)

        IDX = _pattern_index()

        def _blk(reg, p, seq, b0, nb):
            if p == 0:
                return reg[:, b0 * 128:(b0 + nb) * 128]
            if p == 1:
                return reg[:, 512 * b0 + seq:512 * (b0 + nb):4]
            assert nb == 1 and b0 == 0
            return reg[:, seq:ST:16]

        def q_blocks(buf, base, p, seq, b0, nb):
            return _blk(buf[0:71, base:base + ST], p, seq, b0, nb)

        def o_blocks(buf, r0, r1, p, seq, b0, nb):
            return _blk(buf[r0:r1, 0:ST], p, seq, b0, nb)

        def ps_view(ap2d, p, nb):
            return ap2d

        SEQS = [(1, 16), (4, 4), (16, 1)]

        def vblock(p, seq, kb, nbs):
            if p == 0:
                return 15 if kb < 0 else 16 + kb
            if p == 1:
                return (12 + seq) if kb < 0 else 16 + 4 * kb + seq
            return seq if kb < 0 else 16 + seq

        stn = [0]
        vbn = [0]
        obn = [0]
        accn = [0]
        tmpn = [0]
        allkeys = {}

        def stage_b(l, s):
            SB_BANKS = (0, 1, 2, 7)
            norm_q = []
            vbase = vbn[0]
            vsteps = [(h_, p_) for h_ in range(H) for p_ in range(3)]

            def emit_vload(i):
                h_, p_ = vsteps[i]
                vi_ = (vbase + i) % 2
                srcv = VD[p_][(s - 1) * ST:(s + 1) * ST, (h_ // 2) * PAIRW:(h_ // 2 + 1) * PAIRW].rearrange(
                    "(b e) w -> e b w", e=128)
                em.dma(lambda e: e.dma_start(out=VB[vi_], in_=srcv),
                       reads=[f'VD{p_}_{s - 1}', f'VD{p_}_{s}'], writes=[f'VB{vi_}'])
            emit_vload(0)
            OB_BANKS = (3, 5, 6)
            for h in range(H):
                pair, odd = h // 2, h % 2
                hb = h % 2
                QTH_, KTH_ = QTHS[hb], KTHS[hb]
                qk_, kk_ = f'QTH{hb}', f'KTH{hb}'
                em.dma(lambda e, h=h, QTH_=QTH_: e.dma_start(out=QTH_[0:64, :], in_=QT[:, h, :]), reads=['QTd'], writes=[qk_])
                em.dma(lambda e, h=h, QTH_=QTH_: e.dma_start(out=QTH_[64:71, :], in_=augq[h, :, s * ST:(s + 1) * ST]), writes=[qk_])
                em.dma(lambda e, h=h, KTH_=KTH_: e.dma_start(out=KTH_[0:64, :], in_=KT[:, h, (s - 1) * ST:(s + 1) * ST]),
                       reads=[f'KTd{s - 1}', f'KTd{s}'], writes=[kk_])
                em.dma(lambda e, h=h, KTH_=KTH_: e.dma_start(out=KTH_[64:71, :], in_=augk[h, :, (s - 1) * ST:(s + 1) * ST]), writes=[kk_])
                hr = slice(64, 128) if odd else slice(0, 64)
                szk = 'SZPo' if odd else 'SZPe'
                em.dma(lambda e, pair=pair, hr=hr: e.dma_start(out=SZP[hr, :], in_=SZ[hr, pair, :]), reads=['SZd'], writes=[szk])
                OA, OK_, orow = (OO, 'OO', 128) if odd else (OE, 'OE', 65)
                em.op('pool', lambda e: e.memset(RD[0:1, 0:1], 0.0), writes=[OK_] + allkeys.get(OK_, []))
                tasks = []
                for p in range(3):
                    nseq, nbs = SEQS[p]
                    vi = vbn[0] % 2
                    gstep = vbn[0] - vbase
                    vbn[0] += 1
                    pos = 0
                    for seq in range(nseq):
                        for kb in range(-1, nbs):
                            tasks.append((p, seq, kb, vi, nbs, gstep, pos))
                            pos += 1
                info = {}
                addkeys = [[], [], []]

                def emit_st(ti):
                    p, seq, kb, vi, nbs, gstep, pos = tasks[ti]
                    if pos == 4 and gstep + 1 < len(vsteps):
                        emit_vload(gstep + 1)
                    qb0 = max(kb, 0)
                    qb1 = min(kb + 1, nbs - 1)
                    nb = qb1 - qb0 + 1
                    n = nb * 128
                    if kb < 0:
                        kap = q_blocks(KTH_, 0, p, seq, nbs - 1, 1)
                        mcol = 128
                    else:
                        kap = q_blocks(KTH_, ST, p, seq, kb, 1)
                        mcol = 0
                    qap = q_blocks(QTH_, 0, p, seq, qb0, nb)
                    si = stn[0] % 4
                    stn[0] += 1
                    psS = pf(SB_BANKS[si])[:, 0:n]
                    sk = f'PSS{si}'
                    em.op('pe', lambda e: e.matmul(psS, lhsT=kap, rhs=qap, start=True, stop=False),
                          reads=[kk_, qk_], writes=[sk])
                    em.op('pe', lambda e: e.matmul(psS, lhsT=IDN, rhs=MB[:, p, mcol:mcol + n], start=False, stop=True),
                          reads=['IDN', 'MB'], writes=[sk])
                    pt = PT[si][:, 0:n]
                    pk = f'PT{si}'
                    em.op('act', lambda e: e.activation(out=pt, in_=psS, func=AF.Exp), reads=[sk], writes=[pk])
                    info[ti] = (PT[si], pk)

                def emit_pv(ti):
                    p, seq, kb, vi, nbs, gstep, pos = tasks[ti]
                    if kb < 0:
                        return
                    ppt, ppk = info[ti - 1]
                    cpt, cpk = info[ti]
                    prev_ap = ppt[:, 128:256] if kb - 1 >= 0 else ppt[:, 0:128]
                    cur_ap = cpt[:, 0:128]
                    oi = obn[0] % 3
                    obn[0] += 1
                    ok = f'PSO{oi}'
                    vbp = vblock(p, seq, kb - 1, nbs)
                    vbc = vblock(p, seq, kb, nbs)
                    if odd:
                        lhsp, lhsc = VB[vi][:, vbp, 32:160], VB[vi][:, vbc, 32:160]
                        psO = pf(OB_BANKS[oi])[:, 0:128]
                    else:
                        lhsp, lhsc = VB[vi][:, vbp, 0:65], VB[vi][:, vbc, 0:65]
                        psO = pf(OB_BANKS[oi])[0:65, 0:128]
                    em.op('pe', lambda e: e.matmul(psO, lhsT=lhsp, rhs=prev_ap, start=True, stop=False),
                          reads=[f'VB{vi}', ppk], writes=[ok])
                    em.op('pe', lambda e: e.matmul(psO, lhsT=lhsc, rhs=cur_ap, start=False, stop=True),
                          reads=[f'VB{vi}', cpk], writes=[ok])
                    oap = o_blocks(OA, 0, orow, p, seq, kb, 1)
                    akey = f'{OK_}_{p}_{seq}_{kb}'
                    deps = [OK_] + addkeys[p - 1] if p > 0 else [OK_]
                    addkeys[p].append(akey)
                    if p == 0 and accn[0] % 3 == 0:
                        em.op('dve', lambda e: e.tensor_scalar(out=oap, in0=psO, scalar1=1e-30, scalar2=None, op0=ALU.add),
                              reads=[ok] + deps, writes=[akey])
                    elif p == 0:
                        bias_ap = EPSC[0:orow, 2:3]
                        em.op('act', lambda e: e.activation(out=oap, in_=psO, func=AF.Identity, bias=bias_ap, scale=1.0),
                              reads=[ok, 'EPSC'] + deps, writes=[akey])
                    elif accn[0] % 5 in (0, 2):
                        em.op('dve', lambda e: e.tensor_tensor(out=oap, in0=oap, in1=psO, op=ALU.add),
                              reads=[ok] + deps, writes=[akey])
                    else:
                        ti_ = tmpn[0] % 2
                        tmpn[0] += 1
                        tmp = OTMP[ti_][0:orow, :]
                        em.op('act', lambda e: e.activation(out=tmp, in_=psO, func=AF.Copy), reads=[ok], writes=[f'OTMP{ti_}'])
                        em.op('pool', lambda e: e.tensor_tensor(out=oap, in0=oap, in1=tmp, op=ALU.add),
                              reads=[f'OTMP{ti_}'] + deps, writes=[akey])
                    accn[0] += 1

                LA = 2
                for t in range(len(tasks) + LA):
                    if t < len(tasks):
                        emit_st(t)
                    if t - LA >= 0:
                        emit_pv(t - LA)
                    if norm_q and t % 2 == 1 and t >= 5:
                        norm_q.pop(0)()
                while norm_q:
                    norm_q.pop(0)()
                allkeys[OK_] = addkeys[0] + addkeys[1] + addkeys[2]
                row = 32 if odd else 64
                keys_h = list(allkeys[OK_])

                def mk_p1r(c, k, OA=OA, OK_=OK_, row=row, keys_h=keys_h):
                    def f():
                        r = slice(row, row + 1)
                        c0 = c * 512 + k * 128
                        em.op('dve', lambda e: e.reciprocal(out=RD[r, k * 128:(k + 1) * 128], in_=OA[r, c0:c0 + 128]),
                              reads=[OK_] + keys_h, writes=['RD'])
                    return f

                def mk_p1b(c, row=row):
                    def f():
                        r = slice(row, row + 1)
                        em.op('dve', lambda e: e.tensor_copy(out=RH[r, :], in_=RD[r, :]), reads=['RD'], writes=['RH'])
                        em.op('dve', lambda e: e.tensor_tensor(out=RL[r, :], in0=RD[r, :], in1=RH[r, :], op=ALU.subtract),
                              reads=['RD', 'RH'], writes=['RL'])
                    return f

                def mk_p2(c, OA=OA, OK_=OK_, row=row, keys_h=keys_h, odd=odd, pair=pair, hr=hr, szk=szk):
                    def f():
                        cs = slice(c * 512, (c + 1) * 512)
                        r = slice(row, row + 1)
                        if odd:
                            pn = pf(4)[:, :]
                            lhs = SEL[r, 1, :]
                        else:
                            pn = pf(4)[0:64, :]
                            lhs = SEL[r, 0, 0:64]
                        em.op('pe', lambda e: e.matmul(pn, lhsT=lhs, rhs=RH[r, :], start=True, stop=False),
                              reads=['SEL', 'RH'], writes=['PS4n'])
                        em.op('pe', lambda e: e.matmul(pn, lhsT=lhs, rhs=RL[r, :], start=False, stop=True),
                              reads=['SEL', 'RL'], writes=['PS4n'])
                        em.op('dve', lambda e: e.tensor_tensor(out=TMPO[hr, :], in0=OA[hr, cs], in1=pf(4)[hr, :], op=ALU.mult),
                              reads=[OK_, 'PS4n'] + keys_h, writes=['TMPO'])
                        em.op('pool', lambda e: e.tensor_tensor(out=GT[hr, pair, cs], in0=TMPO[hr, :], in1=SZP[hr, cs], op=ALU.mult),
                              reads=['TMPO', szk], writes=['GT'])
                    return f
                seq_ = []
                for c in range(4):
                    seq_ += [mk_p1r(c, k) for k in range(4)]
                    if c > 0:
                        seq_.append(mk_p2(c - 1))
                    seq_.append(mk_p1b(c))
                seq_.append(mk_p2(3))
                norm_q.extend(seq_)
            while norm_q:
                norm_q.pop(0)()

        def stage_c(l, s):
            srcx = xw if l == 0 else X1
            CB = [(XT, 'XT'), (XT1, 'XT1'), (YSB, 'YSB')]

            def ld(j):
                buf, key = CB[j % 3]
                gt = s * NT + j
                r0 = gt * 128
                em.dma(lambda e: e.dma_start(out=buf, in_=srcx[r0:r0 + 128, :]),
                       reads=([f'X1_{gt}'] if l else []), writes=[key])
            ld(0)
            ld(1)
            for j in range(NT):
                if j + 2 < NT:
                    ld(j + 2)
                gt = s * NT + j
                r0 = gt * 128
                buf, key = CB[j % 3]
                banks = (0, 1) if j % 2 == 0 else (5, 6)
                bkeys = ('PC0', 'PC1') if j % 2 == 0 else ('PSO1', 'PSO2')
                for half in range(2):
                    for kc in range(8):
                        lhs = AT[:, kc, j * 128:(j + 1) * 128] if kc < 2 else GT[:, kc - 2, j * 128:(j + 1) * 128]
                        em.op('pe', lambda e, lhs=lhs, kc=kc, half=half, banks=banks: e.matmul(
                            pf(banks[half])[:, :], lhsT=lhs, rhs=Wout[:, kc, half * 512:(half + 1) * 512],
                            start=(kc == 0), stop=(kc == 7)),
                            reads=['AT', 'GT', 'Wout'], writes=[bkeys[half]])
                for half in range(2):
                    em.op('dve', lambda e, half=half, buf=buf, banks=banks: e.tensor_tensor(
                        out=buf[:, half * 512:(half + 1) * 512], in0=buf[:, half * 512:(half + 1) * 512],
                        in1=pf(banks[half])[:, :], op=ALU.add), reads=[key, bkeys[half]], writes=[key])
                if l == 0:
                    em.dma(lambda e, r0=r0, buf=buf: e.dma_start(out=X1[r0:r0 + 128, :], in_=buf), reads=[key], writes=[f'X1_{gt}'])
                else:
                    em.dma(lambda e, j=j, buf=buf: e.dma_start(out=y_o[j * 128:(j + 1) * 128, :], in_=buf), reads=[key])

        if do_sample:
            top = off[0]
            pools = [[o_at, o_at + (2 * ST * 2 + 6 * ST * 2) // 2], [u1, ARENA_ELEMS]]
            _alloc0 = alloc

            def alloc(nbytes):
                n = (nbytes + 63) // 64 * 32
                for pl in pools:
                    if pl[0] + n <= pl[1]:
                        o = pl[0]
                        pl[0] += n
                        return o
                raise AssertionError("sample SBUF overflow")
            o_ = alloc(D * 4); XS = vf(o_, D)
            o_ = alloc(8 * 4 * 2); HTS = vb(o_, 32).rearrange("p (k t) -> p k t", k=8)
            o_ = alloc(768 * 4); QS32 = vf(o_, 768)
            o_ = alloc(3 * 768 * 2); QKVH = vb(o_, 3 * 768).rearrange("p (a c) -> p a c", a=3)
            o_ = alloc(3 * 768 * 2); QKVL = vb(o_, 3 * 768).rearrange("p (a c) -> p a c", a=3)
            QKVBS = []
            for _i in range(2):
                o_ = alloc(3 * 768 * 4); QKVBS.append(vf(o_, 3 * 768).rearrange("p (a c) -> p a c", a=3))
            KCS, VCS, PRDS, VCBS, SCS, PEXS = [], [], [], [], [], []
            for _i in range(2):
                o_ = alloc(768 * 4); KCS.append(vf(o_, 768))
                o_ = alloc(768 * 4); VCS.append(vf(o_, 768))
                o_ = alloc(768 * 4); PRDS.append(vf(o_, 768))
                o_ = alloc(772 * 2); VCBS.append(vb(o_, 772))
                o_ = alloc(16 * 4); SCS.append(vf(o_, 16))
                o_ = alloc(16 * 2); PEXS.append(vb(o_, 16))
            SC = SCS[0]
            o_ = alloc(772 * 4); OSB = vf(o_, 772)
            o_ = alloc(768 * 2); OH = vb(o_, 768)
            o_ = alloc(768 * 2); OL = vb(o_, 768)
            o_ = alloc(3 * H * 4); SBI = vf(o_, 3 * H).rearrange("p (a h) -> p a h", a=3)
            o_ = alloc(768 * 4); BDM = vf(o_, 768)
            o_ = alloc(NSEQ * 128 * 2); SELQ = vb(o_, NSEQ * 128).rearrange("p (i c) -> p i c", i=NSEQ)
            o_ = alloc(NSEQ * NSEQ * 2); SELO = vb(o_, NSEQ * NSEQ).rearrange("p (i c) -> p i c", i=NSEQ)
            o_ = alloc(256 * 4); W00 = vf(o_, 256)
            o_ = alloc(256 * 4); B00 = vf(o_, 256)
            o_ = alloc(D * 2); MIXB = vb(o_, D)
            o_ = alloc(8 * 4 * 2); MIXT = vb(o_, 32).rearrange("p (k t) -> p k t", k=8)
            o_ = alloc(768 * 4); OS32 = vf(o_, 768)
            print("[kernel] sample pools:", pools, "top", top * 2)
            assert pools[1][0] <= top - D * 2 or True

        def sample_layer(l):
            n = NSEQ
            em.dma(lambda e: e.dma_start(out=SBI, in_=sbias), writes=['SBI'])
            em.dma(lambda e: e.dma_start(out=BDM[0:H, :], in_=bdm), writes=['BDM'])
            em.dma(lambda e: e.dma_start(out=SELQ[0:NSEQ], in_=selq_d), writes=['SELQ'])
            em.dma(lambda e: e.dma_start(out=SELO[0:H], in_=selo_d), writes=['SELO'])
            for _i in range(2):
                em.op('pool', lambda e, _i=_i: e.memset(VCBS[_i][:, 768:772], 1.0), writes=[f'VCB{_i}'])
            if l == 0:
                em.dma(lambda e: e.dma_start(out=XS[0:NSEQ, :], in_=xs), writes=['XS'])
            else:
                em.dma(lambda e: e.dma_start(out=XS[0:NSEQ, :], in_=XSD), reads=['XSD'], writes=['XS'])
            em.dma(lambda e: e.dma_start(out=W00[0:n, :], in_=w00[l:l + 1, :].broadcast_to([n, 256])), writes=['W00'])
            em.dma(lambda e: e.dma_start(out=B00[0:n, :], in_=b00[l:l + 1, :].broadcast_to([n, 256])), writes=['B00'])
            rmsnorm_rows(n, XS[0:n, :], 'XS', D, GB[0:n, :], 'GB', HB[0:n, :], 'HB', 0)
            for kc in range(8):
                em.op('pe', lambda e, kc=kc: e.transpose(pb(2)[:, kc * n:(kc + 1) * n],
                                                         HB[0:n, kc * 128:(kc + 1) * 128], IDN[0:n, 0:n]),
                      reads=['HB', 'IDN'], writes=['PS2'])
            em.op('act', lambda e: e.activation(out=HTS.rearrange("p k t -> p (k t)"), in_=pb(2)[:, 0:8 * n], func=AF.Copy),
                  reads=['PS2'], writes=['HTS'])
            inproj_and_evac(n, lambda kc: HTS[:, kc, :], 'HTS', True, pcn)
            qk_norm(n)
            em.dma(lambda e: e.dma_start(out=nks_o[l], in_=K32[0:n, :]), reads=['K32'])
            em.dma(lambda e: e.dma_start(out=nvs_o[l], in_=V32[0:n, :]), reads=['V32'])
            ss = STAT[0:n, 2:3]
            rs = STAT[0:n, 3:4]
            em.op('act', lambda e: e.activation(out=JUNK[0:n, 0:256], in_=UV[0:n, 256:512], func=AF.Square, accum_out=ss),
                  reads=['UV'], writes=['JUNK', 'ST2'])
            em.op('act', lambda e: e.activation(out=rs, in_=ss, func=AF.Sqrt, bias=EPSC[0:n, 0:1], scale=1.0 / 256),
                  reads=['ST2', 'EPSC'], writes=['ST3'])
            em.op('dve', lambda e: e.reciprocal(out=rs, in_=rs), reads=['ST3'], writes=['ST3'])
            em.op('dve', lambda e: e.scalar_tensor_tensor(out=VN32[0:n, :], in0=UV[0:n, 256:512], scalar=rs, in1=SGB[0:n, :],
                                                          op0=ALU.mult, op1=ALU.mult),
                  reads=['UV', 'ST3', 'SGB'], writes=['VN32'])
            em.dma(lambda e: e.dma_start(out=nsg_o[l], in_=VN32[0:n, :]), reads=['VN32'])
            em.op('dve', lambda e: e.tensor_tensor(out=AU[0:n, :], in0=VN32[0:n, :], in1=W00[0:n, :], op=ALU.mult),
                  reads=['VN32', 'W00'], writes=['AU'])
            em.op('dve', lambda e: e.tensor_tensor(out=AU[0:n, :], in0=AU[0:n, :], in1=B00[0:n, :], op=ALU.add),
                  reads=['AU', 'B00'], writes=['AU'])
            em.op('dve', lambda e: e.tensor_tensor(out=AU[0:n, :], in0=AU[0:n, :], in1=UV[0:n, 0:256], op=ALU.mult),
                  reads=['AU', 'UV'], writes=['AU'])
            em.op('dve', lambda e: e.tensor_tensor(out=MIXB[0:n, 0:256], in0=AU[0:n, :], in1=SZA[0:n, :], op=ALU.mult),
                  reads=['AU', 'SZA'], writes=['MIXB'])
            em.op('dve', lambda e: e.tensor_tensor(
                out=QS32[0:n, :].rearrange("p (h d) -> p h d", d=DH), in0=SQ[0:n, 0:768].rearrange("p (h d) -> p h d", d=DH),
                in1=GQ[0:n, :].unsqueeze(1).broadcast_to([n, H, DH]), op=ALU.mult), reads=['SQ', 'GQ'], writes=['QS32'])
            for a, (src, sk) in enumerate(((QS32, 'QS32'), (K32, 'K32'), (V32, 'V32'))):
                em.op('act', lambda e, a=a, src=src: e.activation(out=QKVH[0:n, a, :], in_=src[0:n, :], func=AF.Copy),
                      reads=[sk], writes=['QKVH'])
                em.op('dve', lambda e, a=a, src=src: e.tensor_tensor(out=QKVL[0:n, a, :], in0=src[0:n, :], in1=QKVH[0:n, a, :],
                                                                     op=ALU.subtract), reads=[sk, 'QKVH'], writes=['QKVL'])
            def bcast(i):
                for a in range(3):
                    for (c0, cw) in ((0, 512), (512, 256)):
                        bank = (a * 2 + (c0 // 512)) % 2
                        psb = pf(bank)[:, 0:cw]
                        em.op('pe', lambda e, psb=psb, a=a, c0=c0, cw=cw, i=i: e.matmul(
                            psb, lhsT=SELQ[0:n, i, :], rhs=QKVH[0:n, a, c0:c0 + cw], start=True, stop=False),
                            reads=['SELQ', 'QKVH'], writes=[f'PC{bank}'])
                        em.op('pe', lambda e, psb=psb, a=a, c0=c0, cw=cw, i=i: e.matmul(
                            psb, lhsT=SELQ[0:n, i, :], rhs=QKVL[0:n, a, c0:c0 + cw], start=False, stop=True),
                            reads=['SELQ', 'QKVL'], writes=[f'PC{bank}'])
                        em.op('act', lambda e, psb=psb, a=a, c0=c0, cw=cw, i=i: e.activation(
                            out=QKVBS[i % 2][:, a, c0:c0 + cw], in_=psb, func=AF.Copy), reads=[f'PC{bank}'], writes=[f'QKVB{i % 2}'])

            def seq_body(i):
                for p in range(4):
                    bi = (i * 4 + p) % 2
                    KC, VC, PRD, VCB, SCp, PEX = KCS[bi], VCS[bi], PRDS[bi], VCBS[bi], SCS[bi], PEXS[bi]
                    kKC, kVC, kPRD, kVCB, kSC, kPEX = f'KC{bi}', f'VC{bi}', f'PRD{bi}', f'VCB{bi}', f'SC{bi}', f'PEX{bi}'
                    if p < 3:
                        dil = (1, 4, 16)[p]
                        r0 = CL - 128 * dil
                        em.dma(lambda e, r0=r0, dil=dil, i=i, KC=KC: e.dma_start(out=KC, in_=ck[l, i, r0:CL:dil, :]), writes=[kKC])
                        em.dma(lambda e, r0=r0, dil=dil, i=i, VC=VC: e.dma_start(out=VC, in_=cv[l, i, r0:CL:dil, :]), writes=[kVC])
                        ksrc, kkey = KC, kKC
                        vsrc, vkey = VC, kVC
                    else:
                        ksrc, kkey = QKVBS[i % 2][:, 1, :], f'QKVB{i % 2}'
                        vsrc, vkey = QKVBS[i % 2][:, 2, :], f'QKVB{i % 2}'
                    qsrc = QKVBS[i % 2][:, 0, :]
                    em.op('pool', lambda e, ksrc=ksrc, PRD=PRD, qsrc=qsrc: e.tensor_tensor(out=PRD, in0=ksrc, in1=qsrc, op=ALU.mult),
                          reads=[kkey, f'QKVB{i % 2}'], writes=[kPRD])
                    em.op('dve', lambda e, PRD=PRD, SCp=SCp: e.tensor_reduce(out=SCp[:, 0:H], in_=PRD.rearrange("p (h d) -> p h d", d=DH),
                                                                     axis=AX.X, op=ALU.add), reads=[kPRD], writes=[kSC])
                    if p < 3:
                        em.op('dve', lambda e, p=p, SCp=SCp: e.tensor_tensor(out=SCp[:, 0:H], in0=SCp[:, 0:H], in1=SBI[:, p, :], op=ALU.add),
                              reads=[kSC, 'SBI'], writes=[kSC])
                    em.op('act', lambda e, SCp=SCp: e.activation(out=SCp[:, 0:H], in_=SCp[:, 0:H], func=AF.Exp), reads=[kSC], writes=[kSC])
                    if p == 3:
                        em.op('dve', lambda e, SCp=SCp: e.tensor_scalar(out=SCp[:, 0:H], in0=SCp[:, 0:H], scalar1=3.0, scalar2=None, op0=ALU.mult),
                              reads=[kSC], writes=[kSC])
                    em.op('dve', lambda e, SCp=SCp, PEX=PEX: e.tensor_copy(out=PEX[:, 0:H], in_=SCp[:, 0:H]), reads=[kSC], writes=[kPEX])
                    em.op('act', lambda e, vsrc=vsrc, VCB=VCB: e.activation(out=VCB[:, 0:768], in_=vsrc, func=AF.Copy), reads=[vkey], writes=[kVCB])
                    kp = 128 if p < 3 else 1
                    for (c0, cw, bank) in ((0, 512, 3), (512, 260, 4)):
                        em.op('pe', lambda e, c0=c0, cw=cw, bank=bank, kp=kp, p=p, PEX=PEX, VCB=VCB: e.matmul(
                            pf(bank)[0:H, 0:cw], lhsT=PEX[0:kp, 0:H], rhs=VCB[0:kp, c0:c0 + cw], start=(p == 0), stop=(p == 3)),
                            reads=[kPEX, kVCB], writes=[f'PS{bank}s'])
                em.op('act', lambda e: e.activation(out=OSB[0:H, 0:512], in_=pf(3)[0:H, 0:512], func=AF.Copy), reads=['PS3s'], writes=['OSB'])
                em.op('act', lambda e: e.activation(out=OSB[0:H, 512:772], in_=pf(4)[0:H, 0:260], func=AF.Copy), reads=['PS4s'], writes=['OSB'])
                em.op('dve', lambda e: e.reciprocal(out=SC[0:H, 12:13], in_=OSB[0:H, 768:769]), reads=['OSB'], writes=['SC'])
                em.op('dve', lambda e: e.scalar_tensor_tensor(out=OSB[0:H, 0:768], in0=OSB[0:H, 0:768], scalar=SC[0:H, 12:13],
                                                              in1=BDM[0:H, :], op0=ALU.mult, op1=ALU.mult),
                      reads=['OSB', 'SC', 'BDM'], writes=['OSB'])
                em.op('act', lambda e: e.activation(out=OH[0:H, :], in_=OSB[0:H, 0:768], func=AF.Copy), reads=['OSB'], writes=['OH'])
                em.op('dve', lambda e: e.tensor_tensor(out=OL[0:H, :], in0=OSB[0:H, 0:768], in1=OH[0:H, :], op=ALU.subtract),
                      reads=['OSB', 'OH'], writes=['OL'])
                for (c0, cw, bank) in ((0, 512, 6), (512, 256, 7)):
                    em.op('pe', lambda e, c0=c0, cw=cw, bank=bank, i=i: e.matmul(
                        pf(bank)[0:n, 0:cw], lhsT=SELO[0:H, i, :], rhs=OH[0:H, c0:c0 + cw], start=(i == 0), stop=False),
                        reads=['SELO', 'OH'], writes=[f'PS{bank}o'])
                    em.op('pe', lambda e, c0=c0, cw=cw, bank=bank, i=i: e.matmul(
                        pf(bank)[0:n, 0:cw], lhsT=SELO[0:H, i, :], rhs=OL[0:H, c0:c0 + cw], start=False, stop=(i == n - 1)),
                        reads=['SELO', 'OL'], writes=[f'PS{bank}o'])
            bcast(0)
            for i in range(n):
                if i + 1 < n:
                    bcast(i + 1)
                seq_body(i)
            em.op('act', lambda e: e.activation(out=OS32[0:n, 0:512], in_=pf(6)[0:n, 0:512], func=AF.Copy), reads=['PS6o'], writes=['OS32'])
            em.op('act', lambda e: e.activation(out=OS32[0:n, 512:768], in_=pf(7)[0:n, 0:256], func=AF.Copy), reads=['PS7o'], writes=['OS32'])
            em.op('dve', lambda e: e.tensor_tensor(out=MIXB[0:n, 256:1024], in0=OS32[0:n, :], in1=SZB[0:n, :], op=ALU.mult),
                  reads=['OS32', 'SZB'], writes=['MIXB'])
            for kc in range(8):
                em.op('pe', lambda e, kc=kc: e.transpose(pb(2)[:, kc * n:(kc + 1) * n],
                                                         MIXB[0:n, kc * 128:(kc + 1) * 128], IDN[0:n, 0:n]),
                      reads=['MIXB', 'IDN'], writes=['PS2'])
            em.op('act', lambda e: e.activation(out=MIXT.rearrange("p k t -> p (k t)"), in_=pb(2)[:, 0:8 * n], func=AF.Copy),
                  reads=['PS2'], writes=['MIXT'])
            for half in range(2):
                for kc in range(8):
                    em.op('pe', lambda e, kc=kc, half=half: e.matmul(
                        pf(half)[0:n, :], lhsT=MIXT[:, kc, :], rhs=Wout[:, kc, half * 512:(half + 1) * 512],
                        start=(kc == 0), stop=(kc == 7)), reads=['MIXT', 'Wout'], writes=[f'PC{half}'])
            for half in range(2):
                em.op('dve', lambda e, half=half: e.tensor_tensor(
                    out=XS[0:n, half * 512:(half + 1) * 512], in0=XS[0:n, half * 512:(half + 1) * 512],
                    in1=pf(half)[0:n, :], op=ALU.add), reads=['XS', f'PC{half}'], writes=['XS'])
            if l == nlayers - 1:
                em.dma(lambda e: e.dma_start(out=ys_o, in_=XS[0:n, :]), reads=['XS'])
            else:
                em.dma(lambda e: e.dma_start(out=XSD, in_=XS[0:n, :]), reads=['XS'], writes=['XSD'])

        steps = []
        for l in range(nlayers):
            steps.append(('w', l, 0))
            if do_sample:
                steps.append(('s', l, 0))
            if do_prompt:
                if l == 0:
                    steps.append(('a', 0, 0))
                    for s in (1, 2):
                        steps += [('A', 0, s), ('b', 0, s), ('c', 0, s)]
                else:
                    steps += [('a', 1, 1), ('A', 1, 2), ('b', 1, 2), ('c', 1, 2)]
        if plim is not None:
            steps = steps[:plim]
        print("[kernel] steps:", steps)
        prev_k = None
        for (k, l, s) in steps:
            if not ((prev_k == 'a' and k == 'A') or (prev_k == 'b' and k == 'c')):
                em.barrier()
            prev_k = k
            if k == 'w':
                load_layer_consts(l)
            elif k == 's':
                sample_layer(l)
            elif k == 'a':
                stage_a(l, s, False)
            elif k == 'A':
                stage_a(l, s, True)
            elif k == 'b':
                stage_b(l, s)
            elif k == 'c':
                stage_c(l, s)
        em.barrier()
        em.emit()
    return nc


def _host_consts():
    slopes = _slopes()
    w = TOK_OF_SLOT.astype(np.float32)
    A = slopes[:, None] * w[None, :]
    hi, mid, lo = _split3(A)
    one = np.ones((H, WIN), NPBF)
    idx = _pattern_index()
    maskb = np.zeros((128, 3, 256), np.float32)
    for p in range(3):
        i = idx[p]
        cur = (i[None, :] >= i[:, None])
        prev = (i[:, None] >= i[None, :])
        maskb[:, p, 0:128] = np.where(cur, 0.0, NEG)
        maskb[:, p, 128:256] = np.where(prev, 0.0, NEG)
    ident = np.eye(128, dtype=np.float32).astype(NPBF)
    sel = np.zeros((128, 2, 128), np.float32)
    sel[:, 0, 0:64] = 1.0
    sel[:, 1, 64:128] = 1.0
    ip = I_OF_P
    wmask = (ip[None, :] >= ip[:, None]).astype(np.float32)
    e = np.arange(128)
    sbias = np.zeros((128, 3, H), np.float32)
    for p, dil in enumerate((1, 4, 16)):
        sbias[:, p, :] = -(slopes[None, :] * (dil * (128 - e))[:, None].astype(np.float32))
    bdm = np.zeros((H, 768), np.float32)
    for h in range(H):
        bdm[h, h * 64:(h + 1) * 64] = 1.0
    selq = np.zeros((NSEQ, NSEQ, 128), np.float32)
    selo = np.zeros((H, NSEQ, NSEQ), np.float32)
    for i in range(NSEQ):
        selq[i, i, :] = 1.0
        selo[:, i, i] = 1.0
    return dict(A_hi=hi, A_mid=mid, A_lo=lo, one=one, maskb=maskb.astype(NPBF), ident=ident,
                sel=sel.astype(NPBF), wmask=wmask, sbias=sbias, bdm=bdm,
                selq=selq.astype(NPBF), selo=selo.astype(NPBF))


def _prep(x_prompt, x_sample, cache_k, cache_v, norm_g, w_in, sgu_g, w_spatial, b_spatial,
          q_norm_g, k_norm_g, w_out, cores=None):
    hc = _host_consts()

    ip = I_OF_P
    wsp = np.ascontiguousarray(w_spatial[:, :, ip, :][:, :, :, ip].transpose(0, 3, 1, 2))
    bsp = np.ascontiguousarray(b_spatial[:, :, ip].transpose(0, 2, 1))
    w00 = np.ascontiguousarray(np.repeat(w_spatial[:, :, 0, 0], 64, axis=1))
    b00 = np.ascontiguousarray(np.repeat(b_spatial[:, :, 0], 64, axis=1))
    xp = x_prompt[0]
    cks = cache_k.reshape(2, 32, CL, 768)
    cvs = cache_v.reshape(2, 32, CL, 768)
    in_maps = []
    for c in (range(NCORES) if cores is None else cores):
        pos = ST * (c - 2) + TOK_OF_SLOT
        valid = pos >= 0
        xw = np.zeros((WIN, D), np.float32)
        xw[valid] = xp[pos[valid]]
        kval = np.where(valid, 0.0, NEG).astype(np.float32).astype(NPBF)
        augk = np.stack([hc['A_hi'], hc['A_mid'], hc['A_lo'], hc['one'], hc['one'], hc['one'],
                         np.broadcast_to(kval[None, :], (H, WIN))], axis=1)
        augq = np.stack([hc['one'], hc['one'], hc['one'], -hc['A_hi'], -hc['A_mid'], -hc['A_lo'], hc['one']], axis=1)
        sl = slice(NSEQ * c, NSEQ * (c + 1))
        in_maps.append(dict(
            xw=xw, w_in=w_in, w_out=w_out, norm_g=norm_g, sgu_g=sgu_g, qg=q_norm_g, kg=k_norm_g,
            wsp=wsp, wmask=hc['wmask'], bsp=bsp,
            augk=np.ascontiguousarray(augk), augq=np.ascontiguousarray(augq),
            maskb=hc['maskb'], ident=hc['ident'], sel=hc['sel'],
            xs=np.ascontiguousarray(x_sample[sl, 0, :]),
            ck=np.ascontiguousarray(cks[:, sl]), cv=np.ascontiguousarray(cvs[:, sl]),
            sbias=hc['sbias'], bdm=hc['bdm'], selq=hc['selq'], selo=hc['selo'], w00=w00, b00=b00,
        ))
    return in_maps


_NC_CACHE = {}


def kernel(x_prompt, x_sample, cache_k, cache_v, norm_g, w_in, sgu_g, w_spatial, b_spatial,
           q_norm_g, k_norm_g, w_out):
    f = lambda a: np.ascontiguousarray(np.asarray(a), dtype=np.float32)
    x_prompt, x_sample, cache_k, cache_v = f(x_prompt), f(x_sample), f(cache_k), f(cache_v)
    norm_g, w_in, sgu_g, w_spatial, b_spatial = f(norm_g), f(w_in), f(sgu_g), f(w_spatial), f(b_spatial)
    q_norm_g, k_norm_g, w_out = f(q_norm_g), f(k_norm_g), f(w_out)
    if 'nc' not in _NC_CACHE:
        _NC_CACHE['nc'] = build_nc()
    nc = _NC_CACHE['nc']
    in_maps = _prep(x_prompt, x_sample, cache_k, cache_v, norm_g, w_in, sgu_g, w_spatial, b_spatial,
                    q_norm_g, k_norm_g, w_out)
    res = run_bass_kernel_spmd(nc, in_maps, core_ids=list(range(NCORES)))
    R = res.results
    loc = TOK_OF_SLOT[:ST]
    y = np.zeros((1, 16384, D), np.float32)
    for c in range(NCORES):
        yc = np.asarray(R[c]["y"], dtype=np.float32)
        y[0, ST * c + loc] = yc
    ys = np.concatenate([np.asarray(R[c]["ys"], np.float32) for c in range(NCORES)], axis=0).reshape(32, 1, D)
    nk = np.zeros((2, 1, ST, H, DH), np.float32)
    nv = np.zeros((2, 1, ST, H, DH), np.float32)
    nkc = np.asarray(R[NCORES - 1]["nk"], np.float32)
    nvc = np.asarray(R[NCORES - 1]["nv"], np.float32)
    nk[:, 0, loc] = nkc.reshape(2, ST, H, DH)
    nv[:, 0, loc] = nvc.reshape(2, ST, H, DH)
    nks = np.concatenate([np.asarray(R[c]["nks"], np.float32) for c in range(NCORES)], axis=1).reshape(2, 32, 1, H, DH)
    nvs = np.concatenate([np.asarray(R[c]["nvs"], np.float32) for c in range(NCORES)], axis=1).reshape(2, 32, 1, H, DH)
    nsg = np.concatenate([np.asarray(R[c]["nsg"], np.float32) for c in range(NCORES)], axis=1).reshape(2, 32, 1, 256)
    return (y, ys, nk, nv, nks, nvs, nsg)
```

```python
import numpy as np
import ml_dtypes
from contextlib import ExitStack
import concourse.bass as bass
import concourse.mybir as mybir
from concourse.bass_utils import run_bass_kernel_spmd

F32 = mybir.dt.float32
BF16 = mybir.dt.bfloat16
AF = mybir.ActivationFunctionType
ALU = mybir.AluOpType
AX = mybir.AxisListType
NPBF = ml_dtypes.bfloat16

NCORES = 8
D = 1024
PROJ = 3840
H = 12
DH = 64
ST = 2048
NT = 16
WIN = 3 * ST
EPS = 1e-6
NSEQ = 4
CL = 2048
NEG = -30000.0
import os
SKIP = set(os.environ.get('KSKIP', '').split(','))
PAIRW = 160

C_UA, C_VA, C_ZA, C_Q, C_K, C_V, C_ZB = 0, 256, 512, 768, 1536, 2304, 3072


def _slot_of_tok(w):
    return w


TOK = np.arange(WIN)
SLOT = _slot_of_tok(TOK)
TOK_OF_SLOT = np.empty(WIN, np.int64)
TOK_OF_SLOT[SLOT] = TOK
I_OF_P = np.arange(128)


def _split3(x):
    x = x.astype(np.float32)
    hi = x.astype(NPBF)
    r = x - hi.astype(np.float32)
    mid = r.astype(NPBF)
    r2 = r - mid.astype(np.float32)
    lo = r2.astype(NPBF)
    return hi, mid, lo


def _slopes():
    return (2.0 ** (-8.0 * np.arange(1, H + 1, dtype=np.float32) / H)).astype(np.float32)


def _pattern_index():
    e = np.arange(128)
    return [e.copy(), e.copy(), e.copy()]


class Em:
    def __init__(self, nc, ndma=24):
        self.nc = nc
        self.engs = ['pe', 'act', 'dve', 'pool', 'sp']
        self.ops = {e: [] for e in self.engs}
        self.cnt = {e: 0 for e in self.engs}
        self.ndma = ndma
        self.dcnt = [0] * ndma
        self.dnext = 0
        self.lastw = {}
        self.readers = {}
        self.waited = {e: {} for e in self.engs}
        self.pending = {e: {} for e in self.engs}
        self.alias = {}

    def _exp(self, keys):
        out = []
        for k in keys:
            out.extend(self.alias.get(k, [k]))
        return out

    def _excl(self, reads, writes):
        reads = self._exp(reads)
        writes = self._exp(writes)
        ps = [k for k in reads if isinstance(k, str) and k[0] == 'B' and len(k) <= 3 and k[1].isdigit()]
        if ps:
            reads = [k for k in reads if k not in ps]
            writes = list(writes) + ps
        return reads, writes

    def barrier(self):
        for e in self.engs:
            for o in ['pe', 'act', 'dve', 'pool']:
                if o != e and self.cnt[o] > 0:
                    self.pending[e][o] = self.cnt[o]
            for i in range(self.ndma):
                if self.dcnt[i] > 0:
                    self.pending[e][('d', i)] = 16 * self.dcnt[i]

    def _deps(self, eng, reads, writes):
        reads, writes = self._excl(reads, writes)
        toks = list(self.pending[eng].items())
        self.pending[eng] = {}
        for r in reads:
            if r in self.lastw:
                toks.append(self.lastw[r])
        for w in writes:
            if w in self.lastw:
                toks.append(self.lastw[w])
            toks.extend(self.readers.get(w, []))
        waits = {}
        for (k, v) in toks:
            if k == 'pe' and eng == 'pe':
                continue
            if self.waited[eng].get(k, 0) >= v:
                continue
            waits[k] = max(waits.get(k, 0), v)
        for k, v in waits.items():
            self.waited[eng][k] = v
        return waits

    def _commit(self, tok, reads, writes):
        reads, writes = self._excl(reads, writes)
        for r in reads:
            self.readers.setdefault(r, []).append(tok)
        for w in writes:
            self.lastw[w] = tok
            self.readers[w] = []

    def op(self, eng, fn, reads=(), writes=()):
        waits = self._deps(eng, reads, writes)
        self.cnt[eng] += 1
        tok = (eng, self.cnt[eng])
        self.ops[eng].append((fn, list(waits.items()), (eng, 1)))
        self._commit(tok, reads, writes)

    def dma(self, fn, reads=(), writes=(), q='sp'):
        i = self.dnext
        self.dnext = (self.dnext + 1) % self.ndma
        waits = self._deps(q, reads, writes)
        k = ('d', i)
        if self.dcnt[i] > 0 and self.waited[q].get(k, 0) < 16 * self.dcnt[i]:
            waits[k] = 16 * self.dcnt[i]
            self.waited[q][k] = 16 * self.dcnt[i]
        self.dcnt[i] += 1
        tok = (k, 16 * self.dcnt[i])
        self.ops[q].append((fn, list(waits.items()), (k, 16)))
        self._commit(tok, reads, writes)

    def emit(self):
        nc = self.nc
        with ExitStack() as st:
            sem = {}
            for e in ['pe', 'act', 'dve', 'pool']:
                sem[e] = st.enter_context(nc.semaphore(f"tl_{e}"))
            for i in range(self.ndma):
                sem[('d', i)] = st.enter_context(nc.semaphore(f"dq{i}"))
            block = st.enter_context(nc.Block())
            final = [(('d', i), 16 * self.dcnt[i]) for i in range(self.ndma) if self.dcnt[i] > 0]

            def run(engh, lst, is_sp=False):
                for fn, waits, (k, amt) in lst:
                    for (wk, wv) in waits:
                        engh.wait_ge(sem[wk], wv)
                    ins = fn(engh)
                    ins.then_inc(sem[k], amt)
                if is_sp:
                    for (wk, wv) in final:
                        engh.wait_ge(sem[wk], wv)
                    for e in ['pe', 'act', 'dve', 'pool']:
                        if self.cnt[e] > 0:
                            engh.wait_ge(sem[e], self.cnt[e])

            @block.sync
            def _(e):
                run(e, self.ops['sp'], True)

            @block.tensor
            def _(e):
                run(e, self.ops['pe'])

            @block.scalar
            def _(e):
                run(e, self.ops['act'])

            @block.vector
            def _(e):
                run(e, self.ops['dve'])

            @block.gpsimd
            def _(e):
                run(e, self.ops['pool'])


def build_nc(do_prompt=True, do_sample=True, nlayers=2, plim=None):
    nc = bass.Bass("TRN2", target_bir_lowering=False)

    def din(name, shape, dt=F32):
        return nc.dram_tensor(name, list(shape), dt, kind="ExternalInput").ap()

    def dout(name, shape, dt=F32):
        return nc.dram_tensor(name, list(shape), dt, kind="ExternalOutput").ap()

    def dscr(name, shape, dt):
        return nc.dram_tensor(name, list(shape), dt, kind="Internal").ap()

    xw = din("xw", [WIN, D])
    w_in = din("w_in", [2, D, PROJ])
    w_out = din("w_out", [2, D, D])
    norm_g = din("norm_g", [2, D])
    sgu_g = din("sgu_g", [2, 256])
    qg = din("qg", [2, DH])
    kg = din("kg", [2, DH])
    wsp = din("wsp", [2, 128, 4, 128])
    wmask = din("wmask", [128, 128])
    bsp = din("bsp", [2, 128, 4])
    augk = din("augk", [H, 7, WIN], BF16)
    augq = din("augq", [H, 7, WIN], BF16)
    maskb = din("maskb", [128, 3, 256], BF16)
    ident_d = din("ident", [128, 128], BF16)
    sel_d = din("sel", [128, 2, 128], BF16)
    xs = din("xs", [NSEQ, D])
    ck = din("ck", [2, NSEQ, CL, 768])
    cv = din("cv", [2, NSEQ, CL, 768])
    sbias = din("sbias", [128, 3, H])
    bdm = din("bdm", [H, 768])
    selq_d = din("selq", [NSEQ, NSEQ, 128], BF16)
    selo_d = din("selo", [H, NSEQ, NSEQ], BF16)
    w00 = din("w00", [2, 256])
    b00 = din("b00", [2, 256])

    y_o = dout("y", [ST, D])
    nk_o = dout("nk", [2, ST, 768])
    nv_o = dout("nv", [2, ST, 768])
    ys_o = dout("ys", [NSEQ, D])
    nks_o = dout("nks", [2, NSEQ, 768])
    nvs_o = dout("nvs", [2, NSEQ, 768])
    nsg_o = dout("nsg", [2, NSEQ, 256])

    QT = dscr("QT", [DH, H, ST], BF16)
    KT = dscr("KT", [DH, H, WIN], BF16)
    VD = [dscr(f"VD{p}", [WIN, 6 * PAIRW], BF16) for p in range(3)]
    SZ = dscr("SZ", [128, 6, ST], BF16)
    X1 = dscr("X1", [WIN, D], F32)
    XSD = dscr("XSD", [NSEQ, D], F32)

    em = Em(nc)
    em.alias.update({
        'PC0': ['B0'], 'PC1': ['B1'],
        'PSS0': ['B0'], 'PSS1': ['B1'], 'PSS2': ['B2'], 'PSS3': ['B7'],
        'PSO0': ['B3'], 'PSO1': ['B5'], 'PSO2': ['B6'],
        'PS2': ['B2'],
        'PS3a': ['B3'], 'PS3s': ['B3'],
        'PS4a': ['B4'], 'PS4k': ['B4'], 'PS4n': ['B4'], 'PS4s': ['B4'],
        'PS5a': ['B5'], 'PS6b': ['B6'], 'PS6o': ['B6'],
        'PS7a': ['B7'], 'PS7b': ['B7'], 'PS7o': ['B7'],
        'Win': [f'Win{kc}_{ci}' for kc in range(8) for ci in range(3)],
        'Wout': [f'Wout{kc}' for kc in range(8)],
    })
    slopes = _slopes()

    with ExitStack() as st:
        ARENA_ELEMS = 103 * 1024 + 256
        arena = st.enter_context(nc.sbuf_tensor("arena", [128, ARENA_ELEMS], BF16))
        off = [0]

        def alloc(nbytes):
            n = (nbytes + 63) // 64 * 32
            o = off[0]
            off[0] += n
            assert off[0] <= ARENA_ELEMS, f"SBUF arena overflow {off[0]*2}"
            return o

        def vb(o, n):
            return arena[:, o:o + n]

        def vf(o, n):
            return arena[:, o:o + 2 * n].bitcast(F32)

        psum = [st.enter_context(nc.psum_tensor(f"ps{i}", [128, 512], F32)) for i in range(8)]

        def pf(i):
            return psum[i][:, :]

        def pb(i):
            return psum[i][:, :].bitcast(BF16)

        o_win = alloc(8 * PROJ * 2)
        Win = vb(o_win, 8 * PROJ).rearrange("p (k c) -> p k c", k=8)
        o_wout = alloc(8 * D * 2)
        Wout = vb(o_wout, 8 * D).rearrange("p (k c) -> p k c", k=8)
        o_gb = alloc(D * 4)
        GB = vf(o_gb, D)
        o_c = alloc(64 * 4); GQ = vf(o_c, 64)
        o_c = alloc(64 * 4); GK = vf(o_c, 64)
        o_c = alloc(256 * 4); SGB = vf(o_c, 256)
        o_c = alloc(128 * 2); IDN = vb(o_c, 128)
        o_c = alloc(256 * 2); SEL = vb(o_c, 256).rearrange("p (a b) -> p a b", a=2)
        o_c = alloc(768 * 2); MB = vb(o_c, 768).rearrange("p (a b) -> p a b", a=3)
        o_c = alloc(512 * 4); WSPF = vf(o_c, 512)
        o_c = alloc(128 * 4); WMF = vf(o_c, 128)
        o_c = alloc(512 * 2); WSPB = vb(o_c, 512).rearrange("p (g t) -> p g t", g=4)
        o_c = alloc(4 * 4); BSP = vf(o_c, 4)
        o_c = alloc(4 * 4); EPSC = vf(o_c, 4)
        o_at = alloc(2 * ST * 2)
        AT = vb(o_at, 2 * ST).rearrange("p (c s) -> p c s", c=2)
        o_gt = alloc(6 * ST * 2)
        GT = vb(o_gt, 6 * ST).rearrange("p (c s) -> p c s", c=6)
        o_xt = alloc(D * 4); XT = vf(o_xt, D)
        o_xt1 = alloc(D * 4); XT1 = vf(o_xt1, D)
        o_ysb = alloc(D * 4); YSB = vf(o_ysb, D)
        o_va = alloc(6 * PAIRW * 2); VAUG = vb(o_va, 6 * PAIRW).rearrange("p (c w) -> p c w", c=6)
        u0 = off[0]
        o_junk = alloc(D * 2); JUNK = vb(o_junk, D)
        o_h = alloc(D * 2); HB = vb(o_h, D)
        o_uv = alloc(512 * 4); UV = vf(o_uv, 512)
        o_sza = alloc(256 * 4); SZA = vf(o_sza, 256)
        o_qk = alloc(1536 * 4); QK = vf(o_qk, 1536)
        o_sq = alloc(1536 * 4); SQ = vf(o_sq, 1536)
        o_qa = alloc(768 * 2); QA = vb(o_qa, 768)
        o_ka = alloc(768 * 2); KA = vb(o_ka, 768)
        o_k32 = alloc(768 * 4); K32 = vf(o_k32, 768)
        o_v32 = alloc(768 * 4); V32 = vf(o_v32, 768)
        o_szb = alloc(768 * 2); SZB = vb(o_szb, 768)
        o_vn = alloc(256 * 4); VN32 = vf(o_vn, 256)
        o_au = alloc(256 * 4); AU = vf(o_au, 256)
        o_st = alloc(64 * 4); STAT = vf(o_st, 64)
        u1 = off[0]
        o_ = alloc(D * 2); JUNK1 = vb(o_, D)
        o_ = alloc(D * 2); HB1 = vb(o_, D)
        o_ = alloc(512 * 4); UV1 = vf(o_, 512)
        o_ = alloc(256 * 4); SZA1 = vf(o_, 256)
        o_ = alloc(1536 * 4); QK1 = vf(o_, 1536)
        o_ = alloc(768 * 4); V321 = vf(o_, 768)
        o_ = alloc(768 * 2); SZB1 = vb(o_, 768)
        o_ = alloc(64 * 4); STAT1 = vf(o_, 64)
        o_ = alloc(D * 2); HT1 = vb(o_, D).rearrange("p (k t) -> p k t", k=8)
        o_ht = alloc(D * 2); HT = vb(o_ht, D).rearrange("p (k t) -> p k t", k=8)
        o_vnb = alloc(256 * 2); VNB = vb(o_vnb, 256)
        o_ag = alloc(256 * 2); AGB = vb(o_ag, 256)
        QTB = 2
        o_qst = alloc(H * 128 * QTB * 2); QST = vb(o_qst, H * 128 * QTB).rearrange("p (h s) -> p h s", h=H)
        o_kst = alloc(H * 128 * QTB * 2); KST = vb(o_kst, H * 128 * QTB).rearrange("p (h s) -> p h s", h=H)
        o_zst = alloc(6 * 128 * QTB * 2); ZST = vb(o_zst, 6 * 128 * QTB).rearrange("p (h s) -> p h s", h=6)
        endA = off[0]
        off[0] = u0
        QTHS, KTHS = [], []
        for i in range(2):
            o_qth = alloc(ST * 2); QTHS.append(vb(o_qth, ST))
            o_kth = alloc(2 * ST * 2); KTHS.append(vb(o_kth, 2 * ST))
        VB = []
        for p in range(2):
            o_v = alloc(32 * PAIRW * 2)
            VB.append(vb(o_v, 32 * PAIRW).rearrange("p (b w) -> p b w", b=32))
        NPT = 4
        PT = []
        for i in range(NPT):
            o_p = alloc(256 * 2)
            PT.append(vb(o_p, 256))
        OTMP = []
        for i in range(2):
            o_ = alloc(128 * 4); OTMP.append(vf(o_, 128))
        o_oe = alloc(ST * 4); OE = vf(o_oe, ST)
        o_oo = alloc(ST * 4); OO = vf(o_oo, ST)
        o_szp = alloc(ST * 2); SZP = vb(o_szp, ST)
        o_rd = alloc(512 * 4); RD = vf(o_rd, 512)
        o_rh = alloc(512 * 2); RH = vb(o_rh, 512)
        o_rl = alloc(512 * 2); RL = vb(o_rl, 512)
        o_tmp = alloc(512 * 4); TMPO = vf(o_tmp, 512)
        endB = off[0]
        off[0] = max(endA, endB)
        print("[kernel] SBUF bytes/partition: shared", u0 * 2, "endA", endA * 2, "endB", endB * 2)

        em.dma(lambda e: e.dma_start(out=IDN, in_=ident_d), writes=['IDN'])
        em.dma(lambda e: e.dma_start(out=SEL, in_=sel_d), writes=['SEL'])
        em.dma(lambda e: e.dma_start(out=MB, in_=maskb), writes=['MB'])
        em.dma(lambda e: e.dma_start(out=WMF, in_=wmask), writes=['WMF'])

        def load_layer_consts(l):
            em.dma(lambda e: e.dma_start(out=GB, in_=norm_g[l:l + 1, :].broadcast_to([128, D])), writes=['GB'])
            em.dma(lambda e: e.dma_start(out=GQ, in_=qg[l:l + 1, :].broadcast_to([128, DH])), writes=['GQ'])
            em.dma(lambda e: e.dma_start(out=GK, in_=kg[l:l + 1, :].broadcast_to([128, DH])), writes=['GK'])
            em.dma(lambda e: e.dma_start(out=SGB, in_=sgu_g[l:l + 1, :].broadcast_to([128, 256])), writes=['SGB'])
            em.dma(lambda e: e.dma_start(out=WSPF, in_=wsp[l].rearrange("p g t -> p (g t)")), writes=['WSPF'])
            em.dma(lambda e: e.dma_start(out=BSP, in_=bsp[l]), writes=['BSP'])
            em.op('pool', lambda e: e.tensor_tensor(
                out=WSPB, in0=WSPF.rearrange("p (g t) -> p g t", g=4),
                in1=WMF.unsqueeze(1).broadcast_to([128, 4, 128]), op=ALU.mult),
                reads=['WSPF', 'WMF'], writes=['WSPB'])
            stg = [(QK, 'QK'), (QK1, 'QK1'), (SQ, 'SQ')]
            engs = ['pool', 'act', 'dve']
            n = 0
            for kc in range(8):
                for ci, c0 in enumerate(range(0, PROJ, 1536)):
                    cw = min(1536, PROJ - c0)
                    sb, sk = stg[n % 3]
                    eng = engs[n % 3]
                    n += 1
                    em.dma(lambda e, sb=sb, kc=kc, c0=c0, cw=cw: e.dma_start(
                        out=sb[:, 0:cw], in_=w_in[l, kc * 128:(kc + 1) * 128, c0:c0 + cw]), writes=[sk])
                    if eng == 'act':
                        em.op('act', lambda e, sb=sb, kc=kc, c0=c0, cw=cw: e.activation(
                            out=Win[:, kc, c0:c0 + cw], in_=sb[:, 0:cw], func=AF.Copy), reads=[sk], writes=[f'Win{kc}_{ci}'])
                    else:
                        em.op(eng, lambda e, sb=sb, kc=kc, c0=c0, cw=cw: e.tensor_copy(
                            out=Win[:, kc, c0:c0 + cw], in_=sb[:, 0:cw]), reads=[sk], writes=[f'Win{kc}_{ci}'])
            for kc in range(8):
                sb, sk = stg[n % 3]
                eng = engs[n % 3]
                n += 1
                em.dma(lambda e, sb=sb, kc=kc: e.dma_start(
                    out=sb[:, 0:D], in_=w_out[l, kc * 128:(kc + 1) * 128, :]), writes=[sk])
                if eng == 'act':
                    em.op('act', lambda e, sb=sb, kc=kc: e.activation(
                        out=Wout[:, kc, :], in_=sb[:, 0:D], func=AF.Copy), reads=[sk], writes=[f'Wout{kc}'])
                else:
                    em.op(eng, lambda e, sb=sb, kc=kc: e.tensor_copy(
                        out=Wout[:, kc, :], in_=sb[:, 0:D]), reads=[sk], writes=[f'Wout{kc}'])

        em.op('pool', lambda e: e.memset(EPSC[:, 0:1], EPS), writes=['EPSC'])
        em.op('pool', lambda e: e.memset(EPSC[:, 1:2], DH * EPS), writes=['EPSC'])
        em.op('pool', lambda e: e.memset(EPSC[:, 2:3], 1e-30), writes=['EPSC'])
        em.op('pool', lambda e: e.memset(VAUG, 0.0), writes=['VAUG'])
        em.op('pool', lambda e: e.memset(VAUG[:, :, 64:65], 1.0), writes=['VAUG'])

        class _BS:
            pass
        S0, S1 = _BS(), _BS()
        for S_, sfx, bufs in ((S0, '', (XT, HB, JUNK, UV, SZA, QK, V32, SZB, STAT, HT)),
                              (S1, '1', (XT1, HB1, JUNK1, UV1, SZA1, QK1, V321, SZB1, STAT1, HT1))):
            (S_.XT, S_.HB, S_.JUNK, S_.UV, S_.SZA, S_.QK, S_.V32, S_.SZB, S_.STAT, S_.HT) = bufs
            for nm in ('XT', 'HB', 'JUNK', 'UV', 'SZA', 'QK', 'V32', 'SZB', 'ST', 'HT', 'SSQ', 'PW'):
                setattr(S_, 'k' + nm, nm + sfx)
        BS = [S0, S1]

        def rmsnorm_rows(np_, src, src_key, width, gain, gain_key, out, out_key, stat_col, S=None):
            S = S or S0
            ss = S.STAT[0:np_, stat_col:stat_col + 1]
            rs = S.STAT[0:np_, stat_col + 1:stat_col + 2]
            k0, k1 = f'{S.kST}{stat_col}', f'{S.kST}{stat_col + 1}'
            em.op('act', lambda e: e.activation(out=S.JUNK[0:np_, 0:width], in_=src, func=AF.Square, accum_out=ss),
                  reads=[src_key], writes=[S.kJUNK, k0])
            em.op('act', lambda e: e.activation(out=rs, in_=ss, func=AF.Sqrt, bias=EPSC[0:np_, 0:1], scale=1.0 / width),
                  reads=[k0, 'EPSC'], writes=[k1])
            em.op('dve', lambda e: e.reciprocal(out=rs, in_=rs), reads=[k1], writes=[k1])
            em.op('dve', lambda e: e.scalar_tensor_tensor(out=out, in0=src, scalar=rs, in1=gain,
                                                          op0=ALU.mult, op1=ALU.mult),
                  reads=[src_key, k1, gain_key], writes=[out_key])

        def qk_norm(np_, S=None, konly=False, part=None):
            S = S or S0
            c0 = 768 if konly else 0
            h0 = 12 if konly else 0
            qk = S.QK[0:np_, c0:1536]
            sq = SQ[0:np_, c0:1536]
            ssq = S.STAT[0:np_, 8 + h0:32]
            pw = S.STAT[0:np_, 32 + h0:56]
            nh = 24 - h0
            if part in (None, 'a'):
                em.op('pool', lambda e: e.tensor_tensor(out=sq, in0=qk, in1=qk, op=ALU.mult), reads=[S.kQK], writes=['SQ'])
            if part == 'a':
                return
            if part in (None, 'b', 'b1'):
                em.op('dve', lambda e: e.tensor_reduce(out=ssq, in_=sq.rearrange("p (h d) -> p h d", d=DH),
                                                       axis=AX.X, op=ALU.add), reads=['SQ'], writes=[S.kSSQ])
            if part == 'b1':
                return
            em.op('act', lambda e: e.activation(out=pw, in_=ssq, func=AF.Sqrt, bias=EPSC[0:np_, 1:2], scale=1.0),
                  reads=[S.kSSQ, 'EPSC'], writes=[S.kPW])
            em.op('dve', lambda e: e.reciprocal(out=pw, in_=pw), reads=[S.kPW], writes=[S.kPW])
            pwk = S.STAT[0:np_, 44:56]
            em.op('dve', lambda e: e.tensor_scalar(out=pwk, in0=pwk, scalar1=8.0, scalar2=None,
                                                   op0=ALU.mult), reads=[S.kPW], writes=[S.kPW])
            em.op('dve', lambda e: e.tensor_tensor(
                out=sq.rearrange("p (h d) -> p h d", d=DH), in0=qk.rearrange("p (h d) -> p h d", d=DH),
                in1=pw.unsqueeze(2).broadcast_to([np_, nh, DH]), op=ALU.mult), reads=[S.kQK, S.kPW], writes=['SQ'])
            if not konly:
                em.op('pool', lambda e: e.tensor_tensor(
                    out=QA[0:np_, :].rearrange("p (h d) -> p h d", d=DH),
                    in0=SQ[0:np_, 0:768].rearrange("p (h d) -> p h d", d=DH),
                    in1=GQ[0:np_, :].unsqueeze(1).broadcast_to([np_, H, DH]), op=ALU.mult),
                    reads=['SQ', 'GQ'], writes=['QA'])
            em.op('dve', lambda e: e.tensor_tensor(
                out=K32[0:np_, :].rearrange("p (h d) -> p h d", d=DH),
                in0=SQ[0:np_, 768:1536].rearrange("p (h d) -> p h d", d=DH),
                in1=GK[0:np_, :].unsqueeze(1).broadcast_to([np_, H, DH]), op=ALU.mult),
                reads=['SQ', 'GK'], writes=['K32'])
            if konly:
                em.op('act', lambda e: e.activation(out=KA[0:np_, :], in_=K32[0:np_, :], func=AF.Copy),
                      reads=['K32'], writes=['KA'])
            else:
                em.op('pool', lambda e: e.tensor_copy(out=KA[0:np_, :], in_=K32[0:np_, :]),
                      reads=['K32'], writes=['KA'])

        CH_FULL = [(0, 512), (512, 256), (768, 512), (1280, 256), (1536, 512), (2048, 256),
                   (2304, 512), (2816, 256), (3072, 512), (3584, 256)]
        CH_KV = [(1536, 512), (2048, 256), (2304, 512), (2816, 256)]

        def inproj_and_evac(np_, ht_fn, ht_key, full, pcn, S=None, hook=None, hook_after=None, hook2=None, hook2_after=None):
            S = S or S0
            for ci_, (c0, cw) in enumerate(CH_FULL if full else CH_KV):
                bi_ = pcn[0] % 3
                pcn[0] += 1
                bank = (0, 1, 4)[bi_]
                pk = ('PC0', 'PC1', 'PS4a')[bi_]
                ps = pf(bank)[0:np_, 0:cw]
                for kc in range(8):
                    em.op('pe', lambda e, ps=ps, kc=kc, c0=c0, cw=cw: e.matmul(
                        ps, lhsT=ht_fn(kc), rhs=Win[:, kc, c0:c0 + cw], start=(kc == 0), stop=(kc == 7)),
                        reads=[ht_key, 'Win'], writes=[pk])
                if c0 == 0:
                    em.op('act', lambda e, ps=ps: e.activation(out=S.UV[0:np_, :], in_=ps, func=AF.Copy),
                          reads=[pk], writes=[S.kUV])
                elif c0 == 512:
                    em.op('act', lambda e, ps=ps: e.activation(out=S.SZA[0:np_, :], in_=ps, func=AF.Silu),
                          reads=[pk], writes=[S.kSZA])
                elif c0 in (768, 1280, 1536, 2048):
                    d0 = c0 - 768
                    if full:
                        em.op('dve', lambda e, ps=ps, d0=d0, cw=cw: e.tensor_copy(out=S.QK[0:np_, d0:d0 + cw], in_=ps),
                              reads=[pk], writes=[S.kQK])
                    else:
                        em.op('act', lambda e, ps=ps, d0=d0, cw=cw: e.activation(
                            out=S.QK[0:np_, d0:d0 + cw], in_=ps, func=AF.Copy), reads=[pk], writes=[S.kQK])
                elif c0 in (2304, 2816):
                    d0 = c0 - 2304
                    em.op('act', lambda e, ps=ps, d0=d0, cw=cw: e.activation(
                        out=S.V32[0:np_, d0:d0 + cw], in_=ps, func=AF.Copy), reads=[pk], writes=[S.kV32])
                else:
                    d0 = c0 - 3072
                    em.op('act', lambda e, ps=ps, d0=d0, cw=cw: e.activation(
                        out=S.SZB[0:np_, d0:d0 + cw], in_=ps, func=AF.Silu), reads=[pk], writes=[S.kSZB])
                if hook is not None and ci_ == hook_after:
                    hook()
                if hook2 is not None and ci_ == hook2_after:
                    hook2()

        pcn = [0]

        def stage_a(l, s, full):
            own = (s == 2)
            srcx = xw if l == 0 else X1

            def L(j):
                S = BS[j % 2]
                gt = s * NT + j
                r0 = gt * 128
                em.dma(lambda e, r0=r0: e.dma_start(out=S.XT, in_=srcx[r0:r0 + 128, :]),
                       reads=([f'X1_{gt}'] if l else []), writes=[S.kXT])

            def F1(j):
                S = BS[j % 2]
                rmsnorm_rows(128, S.XT, S.kXT, D, GB, 'GB', S.HB, S.kHB, 0, S)

            def F2(j):
                S = BS[j % 2]
                for kc in range(8):
                    em.op('pe', lambda e, kc=kc: e.transpose(pb(2)[:, kc * 128:(kc + 1) * 128],
                                                             S.HB[:, kc * 128:(kc + 1) * 128], IDN),
                          reads=[S.kHB, 'IDN'], writes=['PS2'])
                em.op('act', lambda e: e.activation(out=S.HT.rearrange("p k t -> p (k t)"), in_=pb(2), func=AF.Copy),
                      reads=['PS2'], writes=[S.kHT])

            def M(j, hook=None):
                S = BS[j % 2]
                h2 = (lambda: (qk_norm(128, S, konly=(not full), part='a'), qk_norm(128, S, konly=(not full), part='b1')))
                inproj_and_evac(128, lambda kc: S.HT[:, kc, :], S.kHT, full, pcn, S,
                                hook=hook, hook_after=0, hook2=h2, hook2_after=(5 if full else 1))

            def B1b(j):
                qk_norm(128, BS[j % 2], konly=(not full), part='b2')

            def B1(j):
                S = BS[j % 2]
                if full:
                    rmsnorm_rows(128, S.UV[:, 256:512], S.kUV, 256, SGB, 'SGB', VNB, 'VNB', 2, S)
                srcv = S.V32.rearrange("p (c t d) -> p c t d", t=2, d=DH)
                em.op('pool', lambda e: e.tensor_copy(out=VAUG[:, :, 0:64], in_=srcv[:, :, 0, :]),
                      reads=[S.kV32], writes=['VAUG'])
                em.op('pool', lambda e: e.tensor_copy(out=VAUG[:, :, 96:160], in_=srcv[:, :, 1, :]),
                      reads=[S.kV32], writes=['VAUG'])

            def SG(j):
                S = BS[j % 2]
                for g in range(4):
                    em.op('pe', lambda e, g=g: e.matmul(pf(6)[:, g * 64:(g + 1) * 64],
                                                        lhsT=WSPB[:, g, :], rhs=VNB[:, g * 64:(g + 1) * 64],
                                                        start=True, stop=True),
                          reads=['WSPB', 'VNB'], writes=['PS6b'])
                for g in range(4):
                    em.op('dve', lambda e, g=g: e.scalar_tensor_tensor(
                        out=AU[:, g * 64:(g + 1) * 64], in0=pf(6)[:, g * 64:(g + 1) * 64],
                        scalar=BSP[:, g:g + 1], in1=S.UV[:, g * 64:(g + 1) * 64], op0=ALU.add, op1=ALU.mult),
                        reads=['PS6b', 'BSP', S.kUV], writes=['AU'])
                em.op('pool', lambda e: e.tensor_tensor(out=AGB, in0=AU, in1=S.SZA, op=ALU.mult),
                      reads=['AU', S.kSZA], writes=['AGB'])

            def B2(j):
                S = BS[j % 2]
                gt = s * NT + j
                r0 = gt * 128
                bq = j % QTB
                if full:
                    for h in range(8):
                        em.op('pe', lambda e, h=h: e.transpose(
                            pb(3)[0:64, h * 128:(h + 1) * 128], QA[:, h * 64:(h + 1) * 64], IDN),
                            reads=['QA', 'IDN'], writes=['PS3a'])
                    em.op('act', lambda e: e.activation(
                        out=QST[0:64, 0:8, bq * 128:(bq + 1) * 128],
                        in_=pb(3)[0:64, :].rearrange("p (h t) -> p h t", h=8), func=AF.Copy),
                        reads=['PS3a'], writes=['QST'])
                for h in range(8):
                    em.op('pe', lambda e, h=h: e.transpose(
                        pb(5)[0:64, h * 128:(h + 1) * 128], KA[:, h * 64:(h + 1) * 64], IDN),
                        reads=['KA', 'IDN'], writes=['PS5a'])
                em.op('dve', lambda e: e.tensor_copy(
                    out=KST[0:64, 0:8, bq * 128:(bq + 1) * 128],
                    in_=pb(5)[0:64, :].rearrange("p (h t) -> p h t", h=8)), reads=['PS5a'], writes=['KST'])
                if full:
                    for h in range(8, 12):
                        em.op('pe', lambda e, h=h: e.transpose(
                            pb(2)[0:64, (h - 8) * 128:(h - 7) * 128], QA[:, h * 64:(h + 1) * 64], IDN),
                            reads=['QA', 'IDN'], writes=['PS2'])
                for h in range(8, 12):
                    em.op('pe', lambda e, h=h: e.transpose(
                        pb(2)[0:64, (h - 4) * 128:(h - 3) * 128], KA[:, h * 64:(h + 1) * 64], IDN),
                        reads=['KA', 'IDN'], writes=['PS2'])
                if full:
                    em.op('act', lambda e: e.activation(
                        out=QST[0:64, 8:12, bq * 128:(bq + 1) * 128],
                        in_=pb(2)[0:64, 0:512].rearrange("p (h t) -> p h t", h=4), func=AF.Copy),
                        reads=['PS2'], writes=['QST'])
                em.op('dve', lambda e: e.tensor_copy(
                    out=KST[0:64, 8:12, bq * 128:(bq + 1) * 128],
                    in_=pb(2)[0:64, 512:1024].rearrange("p (h t) -> p h t", h=4)), reads=['PS2'], writes=['KST'])
                if bq == QTB - 1:
                    g0 = (gt - (QTB - 1)) * 128
                    l0 = (j - (QTB - 1)) * 128
                    if full:
                        em.dma(lambda e: e.dma_start(out=QT[:, :, l0:l0 + 128 * QTB], in_=QST[0:64, :, :]),
                               reads=['QST'], writes=['QTd'])
                    em.dma(lambda e: e.dma_start(out=KT[:, :, g0:g0 + 128 * QTB], in_=KST[0:64, :, :]),
                           reads=['KST'], writes=[f'KTd{s}'])
                em.dma(lambda e: e.dma_start(out=VD[0][r0:r0 + 128, :], in_=VAUG.rearrange("p c w -> p (c w)")),
                       reads=['VAUG'], writes=[f'VD0_{s}'])
                v2 = VD[1][s * ST:(s + 1) * ST, :].rearrange("(m r jj pp) w -> m jj pp r w", m=4, r=4, jj=4, pp=32)
                em.dma(lambda e: e.dma_start(out=v2[j // 4, j % 4], in_=VAUG.rearrange("p c w -> p (c w)")),
                       reads=['VAUG'], writes=[f'VD1_{s}'])
                v3 = VD[2][s * ST:(s + 1) * ST, :].rearrange("(r jj pp) w -> jj pp r w", r=16, jj=16, pp=8)
                em.dma(lambda e: e.dma_start(out=v3[j], in_=VAUG.rearrange("p c w -> p (c w)")),
                       reads=['VAUG'], writes=[f'VD2_{s}'])
                if own:
                    em.dma(lambda e: e.dma_start(out=nk_o[l, j * 128:(j + 1) * 128, :], in_=K32), reads=['K32'])
                    em.dma(lambda e: e.dma_start(out=nv_o[l, j * 128:(j + 1) * 128, :], in_=S.V32), reads=[S.kV32])
                if not full:
                    return
                for c in range(6):
                    em.op('pe', lambda e, c=c: e.transpose(pb(7)[:, c * 128:(c + 1) * 128],
                                                           S.SZB[:, c * 128:(c + 1) * 128], IDN),
                          reads=[S.kSZB, 'IDN'], writes=['PS7a'])
                for c in range(2):
                    em.op('pe', lambda e, c=c: e.transpose(pb(7)[:, 768 + c * 128:768 + (c + 1) * 128],
                                                           AGB[:, c * 128:(c + 1) * 128], IDN),
                          reads=['AGB', 'IDN'], writes=['PS7a'])
                em.op('act', lambda e: e.activation(
                    out=ZST[:, :, bq * 128:(bq + 1) * 128],
                    in_=pb(7)[:, 0:768].rearrange("p (c t) -> p c t", c=6), func=AF.Copy),
                    reads=['PS7a'], writes=['ZST'])
                em.op('act', lambda e: e.activation(
                    out=AT[:, :, j * 128:(j + 1) * 128],
                    in_=pb(7)[:, 768:1024].rearrange("p (c t) -> p c t", c=2), func=AF.Copy),
                    reads=['PS7a'], writes=['AT'])
                if bq == QTB - 1:
                    l0 = (j - (QTB - 1)) * 128
                    em.dma(lambda e: e.dma_start(out=SZ[:, :, l0:l0 + 128 * QTB], in_=ZST),
                           reads=['ZST'], writes=['SZd'])

            L(0)
            for t in range(NT + 2):
                if t + 1 < NT:
                    L(t + 1)
                if t < NT:
                    F1(t)
                if 0 <= t - 2 < NT:
                    B1(t - 2)
                hk = (lambda t=t: B1b(t - 2)) if 0 <= t - 2 < NT else None
                if 0 <= t - 1 < NT:
                    M(t - 1, hk)
                elif hk is not None:
                    hk()
                if full and 0 <= t - 2 < NT:
                    SG(t - 2)
                if t < NT:
                    F2(t)
                if 0 <= t - 2 < NT:
                    B2(t - 2)

        IDX = _pattern_index()

        def _blk(reg, p, seq, b0, nb):
            if p == 0:
                return reg[:, b0 * 128:(b0 + nb) * 128]
            if p == 1:
                return reg[:, 512 * b0 + seq:512 * (b0 + nb):4]
            assert nb == 1 and b0 == 0
            return reg[:, seq:ST:16]

        def q_blocks(buf, base, p, seq, b0, nb):
            return _blk(buf[0:71, base:base + ST], p, seq, b0, nb)

        def o_blocks(buf, r0, r1, p, seq, b0, nb):
            return _blk(buf[r0:r1, 0:ST], p, seq, b0, nb)

        def ps_view(ap2d, p, nb):
            return ap2d

        SEQS = [(1, 16), (4, 4), (16, 1)]

        def vblock(p, seq, kb, nbs):
            if p == 0:
                return 15 if kb < 0 else 16 + kb
            if p == 1:
                return (12 + seq) if kb < 0 else 16 + 4 * kb + seq
            return seq if kb < 0 else 16 + seq

        stn = [0]
        vbn = [0]
        obn = [0]
        accn = [0]
        tmpn = [0]
        allkeys = {}

        def stage_b(l, s):
            SB_BANKS = (0, 1, 2, 7)
            norm_q = []
            vbase = vbn[0]
            vsteps = [(h_, p_) for h_ in range(H) for p_ in range(3)]

            def emit_vload(i):
                h_, p_ = vsteps[i]
                vi_ = (vbase + i) % 2
                srcv = VD[p_][(s - 1) * ST:(s + 1) * ST, (h_ // 2) * PAIRW:(h_ // 2 + 1) * PAIRW].rearrange(
                    "(b e) w -> e b w", e=128)
                em.dma(lambda e: e.dma_start(out=VB[vi_], in_=srcv),
                       reads=[f'VD{p_}_{s - 1}', f'VD{p_}_{s}'], writes=[f'VB{vi_}'])
            emit_vload(0)
            OB_BANKS = (3, 5, 6)
            for h in range(H):
                pair, odd = h // 2, h % 2
                hb = h % 2
                QTH_, KTH_ = QTHS[hb], KTHS[hb]
                qk_, kk_ = f'QTH{hb}', f'KTH{hb}'
                em.dma(lambda e, h=h, QTH_=QTH_: e.dma_start(out=QTH_[0:64, :], in_=QT[:, h, :]), reads=['QTd'], writes=[qk_])
                em.dma(lambda e, h=h, QTH_=QTH_: e.dma_start(out=QTH_[64:71, :], in_=augq[h, :, s * ST:(s + 1) * ST]), writes=[qk_])
                em.dma(lambda e, h=h, KTH_=KTH_: e.dma_start(out=KTH_[0:64, :], in_=KT[:, h, (s - 1) * ST:(s + 1) * ST]),
                       reads=[f'KTd{s - 1}', f'KTd{s}'], writes=[kk_])
                em.dma(lambda e, h=h, KTH_=KTH_: e.dma_start(out=KTH_[64:71, :], in_=augk[h, :, (s - 1) * ST:(s + 1) * ST]), writes=[kk_])
                hr = slice(64, 128) if odd else slice(0, 64)
                szk = 'SZPo' if odd else 'SZPe'
                em.dma(lambda e, pair=pair, hr=hr: e.dma_start(out=SZP[hr, :], in_=SZ[hr, pair, :]), reads=['SZd'], writes=[szk])
                OA, OK_, orow = (OO, 'OO', 128) if odd else (OE, 'OE', 65)
                em.op('pool', lambda e: e.memset(RD[0:1, 0:1], 0.0), writes=[OK_] + allkeys.get(OK_, []))
                tasks = []
                for p in range(3):
                    nseq, nbs = SEQS[p]
                    vi = vbn[0] % 2
                    gstep = vbn[0] - vbase
                    vbn[0] += 1
                    pos = 0
                    for seq in range(nseq):
                        for kb in range(-1, nbs):
                            tasks.append((p, seq, kb, vi, nbs, gstep, pos))
                            pos += 1
                info = {}
                addkeys = [[], [], []]

                def emit_st(ti):
                    p, seq, kb, vi, nbs, gstep, pos = tasks[ti]
                    if pos == 4 and gstep + 1 < len(vsteps):
                        emit_vload(gstep + 1)
                    qb0 = max(kb, 0)
                    qb1 = min(kb + 1, nbs - 1)
                    nb = qb1 - qb0 + 1
                    n = nb * 128
                    if kb < 0:
                        kap = q_blocks(KTH_, 0, p, seq, nbs - 1, 1)
                        mcol = 128
                    else:
                        kap = q_blocks(KTH_, ST, p, seq, kb, 1)
                        mcol = 0
                    qap = q_blocks(QTH_, 0, p, seq, qb0, nb)
                    si = stn[0] % 4
                    stn[0] += 1
                    psS = pf(SB_BANKS[si])[:, 0:n]
                    sk = f'PSS{si}'
                    em.op('pe', lambda e: e.matmul(psS, lhsT=kap, rhs=qap, start=True, stop=False),
                          reads=[kk_, qk_], writes=[sk])
                    em.op('pe', lambda e: e.matmul(psS, lhsT=IDN, rhs=MB[:, p, mcol:mcol + n], start=False, stop=True),
                          reads=['IDN', 'MB'], writes=[sk])
                    pt = PT[si][:, 0:n]
                    pk = f'PT{si}'
                    em.op('act', lambda e: e.activation(out=pt, in_=psS, func=AF.Exp), reads=[sk], writes=[pk])
                    info[ti] = (PT[si], pk)

                def emit_pv(ti):
                    p, seq, kb, vi, nbs, gstep, pos = tasks[ti]
                    if kb < 0:
                        return
                    ppt, ppk = info[ti - 1]
                    cpt, cpk = info[ti]
                    prev_ap = ppt[:, 128:256] if kb - 1 >= 0 else ppt[:, 0:128]
                    cur_ap = cpt[:, 0:128]
                    oi = obn[0] % 3
                    obn[0] += 1
                    ok = f'PSO{oi}'
                    vbp = vblock(p, seq, kb - 1, nbs)
                    vbc = vblock(p, seq, kb, nbs)
                    if odd:
                        lhsp, lhsc = VB[vi][:, vbp, 32:160], VB[vi][:, vbc, 32:160]
                        psO = pf(OB_BANKS[oi])[:, 0:128]
                    else:
                        lhsp, lhsc = VB[vi][:, vbp, 0:65], VB[vi][:, vbc, 0:65]
                        psO = pf(OB_BANKS[oi])[0:65, 0:128]
                    em.op('pe', lambda e: e.matmul(psO, lhsT=lhsp, rhs=prev_ap, start=True, stop=False),
                          reads=[f'VB{vi}', ppk], writes=[ok])
                    em.op('pe', lambda e: e.matmul(psO, lhsT=lhsc, rhs=cur_ap, start=False, stop=True),
                          reads=[f'VB{vi}', cpk], writes=[ok])
                    oap = o_blocks(OA, 0, orow, p, seq, kb, 1)
                    akey = f'{OK_}_{p}_{seq}_{kb}'
                    deps = [OK_] + addkeys[p - 1] if p > 0 else [OK_]
                    addkeys[p].append(akey)
                    if p == 0 and accn[0] % 3 == 0:
                        em.op('dve', lambda e: e.tensor_scalar(out=oap, in0=psO, scalar1=1e-30, scalar2=None, op0=ALU.add),
                              reads=[ok] + deps, writes=[akey])
                    elif p == 0:
                        bias_ap = EPSC[0:orow, 2:3]
                        em.op('act', lambda e: e.activation(out=oap, in_=psO, func=AF.Identity, bias=bias_ap, scale=1.0),
                              reads=[ok, 'EPSC'] + deps, writes=[akey])
                    elif accn[0] % 5 in (0, 2):
                        em.op('dve', lambda e: e.tensor_tensor(out=oap, in0=oap, in1=psO, op=ALU.add),
                              reads=[ok] + deps, writes=[akey])
                    else:
                        ti_ = tmpn[0] % 2
                        tmpn[0] += 1
                        tmp = OTMP[ti_][0:orow, :]
                        em.op('act', lambda e: e.activation(out=tmp, in_=psO, func=AF.Copy), reads=[ok], writes=[f'OTMP{ti_}'])
                        em.op('pool', lambda e: e.tensor_tensor(out=oap, in0=oap, in1=tmp, op=ALU.add),
                              reads=[f'OTMP{ti_}'] + deps, writes=[akey])
                    accn[0] += 1

                LA = 2
                for t in range(len(tasks) + LA):
                    if t < len(tasks):
                        emit_st(t)
                    if t - LA >= 0:
                        emit_pv(t - LA)
                    if norm_q and t % 2 == 1 and t >= 5:
                        norm_q.pop(0)()
                while norm_q:
                    norm_q.pop(0)()
                allkeys[OK_] = addkeys[0] + addkeys[1] + addkeys[2]
                row = 32 if odd else 64
                keys_h = list(allkeys[OK_])

                def mk_p1r(c, k, OA=OA, OK_=OK_, row=row, keys_h=keys_h):
                    def f():
                        r = slice(row, row + 1)
                        c0 = c * 512 + k * 128
                        em.op('dve', lambda e: e.reciprocal(out=RD[r, k * 128:(k + 1) * 128], in_=OA[r, c0:c0 + 128]),
                              reads=[OK_] + keys_h, writes=['RD'])
                    return f

                def mk_p1b(c, row=row):
                    def f():
                        r = slice(row, row + 1)
                        em.op('dve', lambda e: e.tensor_copy(out=RH[r, :], in_=RD[r, :]), reads=['RD'], writes=['RH'])
                        em.op('dve', lambda e: e.tensor_tensor(out=RL[r, :], in0=RD[r, :], in1=RH[r, :], op=ALU.subtract),
                              reads=['RD', 'RH'], writes=['RL'])
                    return f

                def mk_p2(c, OA=OA, OK_=OK_, row=row, keys_h=keys_h, odd=odd, pair=pair, hr=hr, szk=szk):
                    def f():
                        cs = slice(c * 512, (c + 1) * 512)
                        r = slice(row, row + 1)
                        if odd:
                            pn = pf(4)[:, :]
                            lhs = SEL[r, 1, :]
                        else:
                            pn = pf(4)[0:64, :]
                            lhs = SEL[r, 0, 0:64]
                        em.op('pe', lambda e: e.matmul(pn, lhsT=lhs, rhs=RH[r, :], start=True, stop=False),
                              reads=['SEL', 'RH'], writes=['PS4n'])
                        em.op('pe', lambda e: e.matmul(pn, lhsT=lhs, rhs=RL[r, :], start=False, stop=True),
                              reads=['SEL', 'RL'], writes=['PS4n'])
                        em.op('dve', lambda e: e.tensor_tensor(out=TMPO[hr, :], in0=OA[hr, cs], in1=pf(4)[hr, :], op=ALU.mult),
                              reads=[OK_, 'PS4n'] + keys_h, writes=['TMPO'])
                        em.op('pool', lambda e: e.tensor_tensor(out=GT[hr, pair, cs], in0=TMPO[hr, :], in1=SZP[hr, cs], op=ALU.mult),
                              reads=['TMPO', szk], writes=['GT'])
                    return f
                seq_ = []
                for c in range(4):
                    seq_ += [mk_p1r(c, k) for k in range(4)]
                    if c > 0:
                        seq_.append(mk_p2(c - 1))
                    seq_.append(mk_p1b(c))
                seq_.append(mk_p2(3))
                norm_q.extend(seq_)
            while norm_q:
                norm_q.pop(0)()

        def stage_c(l, s):
            srcx = xw if l == 0 else X1
            CB = [(XT, 'XT'), (XT1, 'XT1'), (YSB, 'YSB')]

            def ld(j):
                buf, key = CB[j % 3]
                gt = s * NT + j
                r0 = gt * 128
                em.dma(lambda e: e.dma_start(out=buf, in_=srcx[r0:r0 + 128, :]),
                       reads=([f'X1_{gt}'] if l else []), writes=[key])
            ld(0)
            ld(1)
            for j in range(NT):
                if j + 2 < NT:
                    ld(j + 2)
                gt = s * NT + j
                r0 = gt * 128
                buf, key = CB[j % 3]
                banks = (0, 1) if j % 2 == 0 else (5, 6)
                bkeys = ('PC0', 'PC1') if j % 2 == 0 else ('PSO1', 'PSO2')
                for half in range(2):
                    for kc in range(8):
                        lhs = AT[:, kc, j * 128:(j + 1) * 128] if kc < 2 else GT[:, kc - 2, j * 128:(j + 1) * 128]
                        em.op('pe', lambda e, lhs=lhs, kc=kc, half=half, banks=banks: e.matmul(
                            pf(banks[half])[:, :], lhsT=lhs, rhs=Wout[:, kc, half * 512:(half + 1) * 512],
                            start=(kc == 0), stop=(kc == 7)),
                            reads=['AT', 'GT', 'Wout'], writes=[bkeys[half]])
                for half in range(2):
                    em.op('dve', lambda e, half=half, buf=buf, banks=banks: e.tensor_tensor(
                        out=buf[:, half * 512:(half + 1) * 512], in0=buf[:, half * 512:(half + 1) * 512],
                        in1=pf(banks[half])[:, :], op=ALU.add), reads=[key, bkeys[half]], writes=[key])
                if l == 0:
                    em.dma(lambda e, r0=r0, buf=buf: e.dma_start(out=X1[r0:r0 + 128, :], in_=buf), reads=[key], writes=[f'X1_{gt}'])
                else:
                    em.dma(lambda e, j=j, buf=buf: e.dma_start(out=y_o[j * 128:(j + 1) * 128, :], in_=buf), reads=[key])

        if do_sample:
            top = off[0]
            pools = [[o_at, o_at + (2 * ST * 2 + 6 * ST * 2) // 2], [u1, ARENA_ELEMS]]
            _alloc0 = alloc

            def alloc(nbytes):
                n = (nbytes + 63) // 64 * 32
                for pl in pools:
                    if pl[0] + n <= pl[1]:
                        o = pl[0]
                        pl[0] += n
                        return o
                raise AssertionError("sample SBUF overflow")
            o_ = alloc(D * 4); XS = vf(o_, D)
            o_ = alloc(8 * 4 * 2); HTS = vb(o_, 32).rearrange("p (k t) -> p k t", k=8)
            o_ = alloc(768 * 4); QS32 = vf(o_, 768)
            o_ = alloc(3 * 768 * 2); QKVH = vb(o_, 3 * 768).rearrange("p (a c) -> p a c", a=3)
            o_ = alloc(3 * 768 * 2); QKVL = vb(o_, 3 * 768).rearrange("p (a c) -> p a c", a=3)
            QKVBS = []
            for _i in range(2):
                o_ = alloc(3 * 768 * 4); QKVBS.append(vf(o_, 3 * 768).rearrange("p (a c) -> p a c", a=3))
            KCS, VCS, PRDS, VCBS, SCS, PEXS = [], [], [], [], [], []
            for _i in range(2):
                o_ = alloc(768 * 4); KCS.append(vf(o_, 768))
                o_ = alloc(768 * 4); VCS.append(vf(o_, 768))
                o_ = alloc(768 * 4); PRDS.append(vf(o_, 768))
                o_ = alloc(772 * 2); VCBS.append(vb(o_, 772))
                o_ = alloc(16 * 4); SCS.append(vf(o_, 16))
                o_ = alloc(16 * 2); PEXS.append(vb(o_, 16))
            SC = SCS[0]
            o_ = alloc(772 * 4); OSB = vf(o_, 772)
            o_ = alloc(768 * 2); OH = vb(o_, 768)
            o_ = alloc(768 * 2); OL = vb(o_, 768)
            o_ = alloc(3 * H * 4); SBI = vf(o_, 3 * H).rearrange("p (a h) -> p a h", a=3)
            o_ = alloc(768 * 4); BDM = vf(o_, 768)
            o_ = alloc(NSEQ * 128 * 2); SELQ = vb(o_, NSEQ * 128).rearrange("p (i c) -> p i c", i=NSEQ)
            o_ = alloc(NSEQ * NSEQ * 2); SELO = vb(o_, NSEQ * NSEQ).rearrange("p (i c) -> p i c", i=NSEQ)
            o_ = alloc(256 * 4); W00 = vf(o_, 256)
            o_ = alloc(256 * 4); B00 = vf(o_, 256)
            o_ = alloc(D * 2); MIXB = vb(o_, D)
            o_ = alloc(8 * 4 * 2); MIXT = vb(o_, 32).rearrange("p (k t) -> p k t", k=8)
            o_ = alloc(768 * 4); OS32 = vf(o_, 768)
            print("[kernel] sample pools:", pools, "top", top * 2)
            assert pools[1][0] <= top - D * 2 or True

        def sample_layer(l):
            n = NSEQ
            em.dma(lambda e: e.dma_start(out=SBI, in_=sbias), writes=['SBI'])
            em.dma(lambda e: e.dma_start(out=BDM[0:H, :], in_=bdm), writes=['BDM'])
            em.dma(lambda e: e.dma_start(out=SELQ[0:NSEQ], in_=selq_d), writes=['SELQ'])
            em.dma(lambda e: e.dma_start(out=SELO[0:H], in_=selo_d), writes=['SELO'])
            for _i in range(2):
                em.op('pool', lambda e, _i=_i: e.memset(VCBS[_i][:, 768:772], 1.0), writes=[f'VCB{_i}'])
            if l == 0:
                em.dma(lambda e: e.dma_start(out=XS[0:NSEQ, :], in_=xs), writes=['XS'])
            else:
                em.dma(lambda e: e.dma_start(out=XS[0:NSEQ, :], in_=XSD), reads=['XSD'], writes=['XS'])
            em.dma(lambda e: e.dma_start(out=W00[0:n, :], in_=w00[l:l + 1, :].broadcast_to([n, 256])), writes=['W00'])
            em.dma(lambda e: e.dma_start(out=B00[0:n, :], in_=b00[l:l + 1, :].broadcast_to([n, 256])), writes=['B00'])
            rmsnorm_rows(n, XS[0:n, :], 'XS', D, GB[0:n, :], 'GB', HB[0:n, :], 'HB', 0)
            for kc in range(8):
                em.op('pe', lambda e, kc=kc: e.transpose(pb(2)[:, kc * n:(kc + 1) * n],
                                                         HB[0:n, kc * 128:(kc + 1) * 128], IDN[0:n, 0:n]),
                      reads=['HB', 'IDN'], writes=['PS2'])
            em.op('act', lambda e: e.activation(out=HTS.rearrange("p k t -> p (k t)"), in_=pb(2)[:, 0:8 * n], func=AF.Copy),
                  reads=['PS2'], writes=['HTS'])
            inproj_and_evac(n, lambda kc: HTS[:, kc, :], 'HTS', True, pcn)
            qk_norm(n)
            em.dma(lambda e: e.dma_start(out=nks_o[l], in_=K32[0:n, :]), reads=['K32'])
            em.dma(lambda e: e.dma_start(out=nvs_o[l], in_=V32[0:n, :]), reads=['V32'])
            ss = STAT[0:n, 2:3]
            rs = STAT[0:n, 3:4]
            em.op('act', lambda e: e.activation(out=JUNK[0:n, 0:256], in_=UV[0:n, 256:512], func=AF.Square, accum_out=ss),
                  reads=['UV'], writes=['JUNK', 'ST2'])
            em.op('act', lambda e: e.activation(out=rs, in_=ss, func=AF.Sqrt, bias=EPSC[0:n, 0:1], scale=1.0 / 256),
                  reads=['ST2', 'EPSC'], writes=['ST3'])
            em.op('dve', lambda e: e.reciprocal(out=rs, in_=rs), reads=['ST3'], writes=['ST3'])
            em.op('dve', lambda e: e.scalar_tensor_tensor(out=VN32[0:n, :], in0=UV[0:n, 256:512], scalar=rs, in1=SGB[0:n, :],
                                                          op0=ALU.mult, op1=ALU.mult),
                  reads=['UV', 'ST3', 'SGB'], writes=['VN32'])
            em.dma(lambda e: e.dma_start(out=nsg_o[l], in_=VN32[0:n, :]), reads=['VN32'])
            em.op('dve', lambda e: e.tensor_tensor(out=AU[0:n, :], in0=VN32[0:n, :], in1=W00[0:n, :], op=ALU.mult),
                  reads=['VN32', 'W00'], writes=['AU'])
            em.op('dve', lambda e: e.tensor_tensor(out=AU[0:n, :], in0=AU[0:n, :], in1=B00[0:n, :], op=ALU.add),
                  reads=['AU', 'B00'], writes=['AU'])
            em.op('dve', lambda e: e.tensor_tensor(out=AU[0:n, :], in0=AU[0:n, :], in1=UV[0:n, 0:256], op=ALU.mult),
                  reads=['AU', 'UV'], writes=['AU'])
            em.op('dve', lambda e: e.tensor_tensor(out=MIXB[0:n, 0:256], in0=AU[0:n, :], in1=SZA[0:n, :], op=ALU.mult),
                  reads=['AU', 'SZA'], writes=['MIXB'])
            em.op('dve', lambda e: e.tensor_tensor(
                out=QS32[0:n, :].rearrange("p (h d) -> p h d", d=DH), in0=SQ[0:n, 0:768].rearrange("p (h d) -> p h d", d=DH),
                in1=GQ[0:n, :].unsqueeze(1).broadcast_to([n, H, DH]), op=ALU.mult), reads=['SQ', 'GQ'], writes=['QS32'])
            for a, (src, sk) in enumerate(((QS32, 'QS32'), (K32, 'K32'), (V32, 'V32'))):
                em.op('act', lambda e, a=a, src=src: e.activation(out=QKVH[0:n, a, :], in_=src[0:n, :], func=AF.Copy),
                      reads=[sk], writes=['QKVH'])
                em.op('dve', lambda e, a=a, src=src: e.tensor_tensor(out=QKVL[0:n, a, :], in0=src[0:n, :], in1=QKVH[0:n, a, :],
                                                                     op=ALU.subtract), reads=[sk, 'QKVH'], writes=['QKVL'])
            def bcast(i):
                for a in range(3):
                    for (c0, cw) in ((0, 512), (512, 256)):
                        bank = (a * 2 + (c0 // 512)) % 2
                        psb = pf(bank)[:, 0:cw]
                        em.op('pe', lambda e, psb=psb, a=a, c0=c0, cw=cw, i=i: e.matmul(
                            psb, lhsT=SELQ[0:n, i, :], rhs=QKVH[0:n, a, c0:c0 + cw], start=True, stop=False),
                            reads=['SELQ', 'QKVH'], writes=[f'PC{bank}'])
                        em.op('pe', lambda e, psb=psb, a=a, c0=c0, cw=cw, i=i: e.matmul(
                            psb, lhsT=SELQ[0:n, i, :], rhs=QKVL[0:n, a, c0:c0 + cw], start=False, stop=True),
                            reads=['SELQ', 'QKVL'], writes=[f'PC{bank}'])
                        em.op('act', lambda e, psb=psb, a=a, c0=c0, cw=cw, i=i: e.activation(
                            out=QKVBS[i % 2][:, a, c0:c0 + cw], in_=psb, func=AF.Copy), reads=[f'PC{bank}'], writes=[f'QKVB{i % 2}'])

            def seq_body(i):
                for p in range(4):
                    bi = (i * 4 + p) % 2
                    KC, VC, PRD, VCB, SCp, PEX = KCS[bi], VCS[bi], PRDS[bi], VCBS[bi], SCS[bi], PEXS[bi]
                    kKC, kVC, kPRD, kVCB, kSC, kPEX = f'KC{bi}', f'VC{bi}', f'PRD{bi}', f'VCB{bi}', f'SC{bi}', f'PEX{bi}'
                    if p < 3:
                        dil = (1, 4, 16)[p]
                        r0 = CL - 128 * dil
                        em.dma(lambda e, r0=r0, dil=dil, i=i, KC=KC: e.dma_start(out=KC, in_=ck[l, i, r0:CL:dil, :]), writes=[kKC])
                        em.dma(lambda e, r0=r0, dil=dil, i=i, VC=VC: e.dma_start(out=VC, in_=cv[l, i, r0:CL:dil, :]), writes=[kVC])
                        ksrc, kkey = KC, kKC
                        vsrc, vkey = VC, kVC
                    else:
                        ksrc, kkey = QKVBS[i % 2][:, 1, :], f'QKVB{i % 2}'
                        vsrc, vkey = QKVBS[i % 2][:, 2, :], f'QKVB{i % 2}'
                    qsrc = QKVBS[i % 2][:, 0, :]
                    em.op('pool', lambda e, ksrc=ksrc, PRD=PRD, qsrc=qsrc: e.tensor_tensor(out=PRD, in0=ksrc, in1=qsrc, op=ALU.mult),
                          reads=[kkey, f'QKVB{i % 2}'], writes=[kPRD])
                    em.op('dve', lambda e, PRD=PRD, SCp=SCp: e.tensor_reduce(out=SCp[:, 0:H], in_=PRD.rearrange("p (h d) -> p h d", d=DH),
                                                                     axis=AX.X, op=ALU.add), reads=[kPRD], writes=[kSC])
                    if p < 3:
                        em.op('dve', lambda e, p=p, SCp=SCp: e.tensor_tensor(out=SCp[:, 0:H], in0=SCp[:, 0:H], in1=SBI[:, p, :], op=ALU.add),
                              reads=[kSC, 'SBI'], writes=[kSC])
                    em.op('act', lambda e, SCp=SCp: e.activation(out=SCp[:, 0:H], in_=SCp[:, 0:H], func=AF.Exp), reads=[kSC], writes=[kSC])
                    if p == 3:
                        em.op('dve', lambda e, SCp=SCp: e.tensor_scalar(out=SCp[:, 0:H], in0=SCp[:, 0:H], scalar1=3.0, scalar2=None, op0=ALU.mult),
                              reads=[kSC], writes=[kSC])
                    em.op('dve', lambda e, SCp=SCp, PEX=PEX: e.tensor_copy(out=PEX[:, 0:H], in_=SCp[:, 0:H]), reads=[kSC], writes=[kPEX])
                    em.op('act', lambda e, vsrc=vsrc, VCB=VCB: e.activation(out=VCB[:, 0:768], in_=vsrc, func=AF.Copy), reads=[vkey], writes=[kVCB])
                    kp = 128 if p < 3 else 1
                    for (c0, cw, bank) in ((0, 512, 3), (512, 260, 4)):
                        em.op('pe', lambda e, c0=c0, cw=cw, bank=bank, kp=kp, p=p, PEX=PEX, VCB=VCB: e.matmul(
                            pf(bank)[0:H, 0:cw], lhsT=PEX[0:kp, 0:H], rhs=VCB[0:kp, c0:c0 + cw], start=(p == 0), stop=(p == 3)),
                            reads=[kPEX, kVCB], writes=[f'PS{bank}s'])
                em.op('act', lambda e: e.activation(out=OSB[0:H, 0:512], in_=pf(3)[0:H, 0:512], func=AF.Copy), reads=['PS3s'], writes=['OSB'])
                em.op('act', lambda e: e.activation(out=OSB[0:H, 512:772], in_=pf(4)[0:H, 0:260], func=AF.Copy), reads=['PS4s'], writes=['OSB'])
                em.op('dve', lambda e: e.reciprocal(out=SC[0:H, 12:13], in_=OSB[0:H, 768:769]), reads=['OSB'], writes=['SC'])
                em.op('dve', lambda e: e.scalar_tensor_tensor(out=OSB[0:H, 0:768], in0=OSB[0:H, 0:768], scalar=SC[0:H, 12:13],
                                                              in1=BDM[0:H, :], op0=ALU.mult, op1=ALU.mult),
                      reads=['OSB', 'SC', 'BDM'], writes=['OSB'])
                em.op('act', lambda e: e.activation(out=OH[0:H, :], in_=OSB[0:H, 0:768], func=AF.Copy), reads=['OSB'], writes=['OH'])
                em.op('dve', lambda e: e.tensor_tensor(out=OL[0:H, :], in0=OSB[0:H, 0:768], in1=OH[0:H, :], op=ALU.subtract),
                      reads=['OSB', 'OH'], writes=['OL'])
                for (c0, cw, bank) in ((0, 512, 6), (512, 256, 7)):
                    em.op('pe', lambda e, c0=c0, cw=cw, bank=bank, i=i: e.matmul(
                        pf(bank)[0:n, 0:cw], lhsT=SELO[0:H, i, :], rhs=OH[0:H, c0:c0 + cw], start=(i == 0), stop=False),
                        reads=['SELO', 'OH'], writes=[f'PS{bank}o'])
                    em.op('pe', lambda e, c0=c0, cw=cw, bank=bank, i=i: e.matmul(
                        pf(bank)[0:n, 0:cw], lhsT=SELO[0:H, i, :], rhs=OL[0:H, c0:c0 + cw], start=False, stop=(i == n - 1)),
                        reads=['SELO', 'OL'], writes=[f'PS{bank}o'])
            bcast(0)
            for i in range(n):
                if i + 1 < n:
                    bcast(i + 1)
                seq_body(i)
            em.op('act', lambda e: e.activation(out=OS32[0:n, 0:512], in_=pf(6)[0:n, 0:512], func=AF.Copy), reads=['PS6o'], writes=['OS32'])
            em.op('act', lambda e: e.activation(out=OS32[0:n, 512:768], in_=pf(7)[0:n, 0:256], func=AF.Copy), reads=['PS7o'], writes=['OS32'])
            em.op('dve', lambda e: e.tensor_tensor(out=MIXB[0:n, 256:1024], in0=OS32[0:n, :], in1=SZB[0:n, :], op=ALU.mult),
                  reads=['OS32', 'SZB'], writes=['MIXB'])
            for kc in range(8):
                em.op('pe', lambda e, kc=kc: e.transpose(pb(2)[:, kc * n:(kc + 1) * n],
                                                         MIXB[0:n, kc * 128:(kc + 1) * 128], IDN[0:n, 0:n]),
                      reads=['MIXB', 'IDN'], writes=['PS2'])
            em.op('act', lambda e: e.activation(out=MIXT.rearrange("p k t -> p (k t)"), in_=pb(2)[:, 0:8 * n], func=AF.Copy),
                  reads=['PS2'], writes=['MIXT'])
            for half in range(2):
                for kc in range(8):
                    em.op('pe', lambda e, kc=kc, half=half: e.matmul(
                        pf(half)[0:n, :], lhsT=MIXT[:, kc, :], rhs=Wout[:, kc, half * 512:(half + 1) * 512],
                        start=(kc == 0), stop=(kc == 7)), reads=['MIXT', 'Wout'], writes=[f'PC{half}'])
            for half in range(2):
                em.op('dve', lambda e, half=half: e.tensor_tensor(
                    out=XS[0:n, half * 512:(half + 1) * 512], in0=XS[0:n, half * 512:(half + 1) * 512],
                    in1=pf(half)[0:n, :], op=ALU.add), reads=['XS', f'PC{half}'], writes=['XS'])
            if l == nlayers - 1:
                em.dma(lambda e: e.dma_start(out=ys_o, in_=XS[0:n, :]), reads=['XS'])
            else:
                em.dma(lambda e: e.dma_start(out=XSD, in_=XS[0:n, :]), reads=['XS'], writes=['XSD'])

        steps = []
        for l in range(nlayers):
            steps.append(('w', l, 0))
            if do_sample:
                steps.append(('s', l, 0))
            if do_prompt:
                if l == 0:
                    steps.append(('a', 0, 0))
                    for s in (1, 2):
                        steps += [('A', 0, s), ('b', 0, s), ('c', 0, s)]
                else:
                    steps += [('a', 1, 1), ('A', 1, 2), ('b', 1, 2), ('c', 1, 2)]
        if plim is not None:
            steps = steps[:plim]
        print("[kernel] steps:", steps)
        prev_k = None
        for (k, l, s) in steps:
            if not ((prev_k == 'a' and k == 'A') or (prev_k == 'b' and k == 'c')):
                em.barrier()
            prev_k = k
            if k == 'w':
                load_layer_consts(l)
            elif k == 's':
                sample_layer(l)
            elif k == 'a':
                stage_a(l, s, False)
            elif k == 'A':
                stage_a(l, s, True)
            elif k == 'b':
                stage_b(l, s)
            elif k == 'c':
                stage_c(l, s)
        em.barrier()
        em.emit()
    return nc


def _host_consts():
    slopes = _slopes()
    w = TOK_OF_SLOT.astype(np.float32)
    A = slopes[:, None] * w[None, :]
    hi, mid, lo = _split3(A)
    one = np.ones((H, WIN), NPBF)
    idx = _pattern_index()
    maskb = np.zeros((128, 3, 256), np.float32)
    for p in range(3):
        i = idx[p]
        cur = (i[None, :] >= i[:, None])
        prev = (i[:, None] >= i[None, :])
        maskb[:, p, 0:128] = np.where(cur, 0.0, NEG)
        maskb[:, p, 128:256] = np.where(prev, 0.0, NEG)
    ident = np.eye(128, dtype=np.float32).astype(NPBF)
    sel = np.zeros((128, 2, 128), np.float32)
    sel[:, 0, 0:64] = 1.0
    sel[:, 1, 64:128] = 1.0
    ip = I_OF_P
    wmask = (ip[None, :] >= ip[:, None]).astype(np.float32)
    e = np.arange(128)
    sbias = np.zeros((128, 3, H), np.float32)
    for p, dil in enumerate((1, 4, 16)):
        sbias[:, p, :] = -(slopes[None, :] * (dil * (128 - e))[:, None].astype(np.float32))
    bdm = np.zeros((H, 768), np.float32)
    for h in range(H):
        bdm[h, h * 64:(h + 1) * 64] = 1.0
    selq = np.zeros((NSEQ, NSEQ, 128), np.float32)
    selo = np.zeros((H, NSEQ, NSEQ), np.float32)
    for i in range(NSEQ):
        selq[i, i, :] = 1.0
        selo[:, i, i] = 1.0
    return dict(A_hi=hi, A_mid=mid, A_lo=lo, one=one, maskb=maskb.astype(NPBF), ident=ident,
                sel=sel.astype(NPBF), wmask=wmask, sbias=sbias, bdm=bdm,
                selq=selq.astype(NPBF), selo=selo.astype(NPBF))


def _prep(x_prompt, x_sample, cache_k, cache_v, norm_g, w_in, sgu_g, w_spatial, b_spatial,
          q_norm_g, k_norm_g, w_out, cores=None):
    hc = _host_consts()

    ip = I_OF_P
    wsp = np.ascontiguousarray(w_spatial[:, :, ip, :][:, :, :, ip].transpose(0, 3, 1, 2))
    bsp = np.ascontiguousarray(b_spatial[:, :, ip].transpose(0, 2, 1))
    w00 = np.ascontiguousarray(np.repeat(w_spatial[:, :, 0, 0], 64, axis=1))
    b00 = np.ascontiguousarray(np.repeat(b_spatial[:, :, 0], 64, axis=1))
    xp = x_prompt[0]
    cks = cache_k.reshape(2, 32, CL, 768)
    cvs = cache_v.reshape(2, 32, CL, 768)
    in_maps = []
    for c in (range(NCORES) if cores is None else cores):
        pos = ST * (c - 2) + TOK_OF_SLOT
        valid = pos >= 0
        xw = np.zeros((WIN, D), np.float32)
        xw[valid] = xp[pos[valid]]
        kval = np.where(valid, 0.0, NEG).astype(np.float32).astype(NPBF)
        augk = np.stack([hc['A_hi'], hc['A_mid'], hc['A_lo'], hc['one'], hc['one'], hc['one'],
                         np.broadcast_to(kval[None, :], (H, WIN))], axis=1)
        augq = np.stack([hc['one'], hc['one'], hc['one'], -hc['A_hi'], -hc['A_mid'], -hc['A_lo'], hc['one']], axis=1)
        sl = slice(NSEQ * c, NSEQ * (c + 1))
        in_maps.append(dict(
            xw=xw, w_in=w_in, w_out=w_out, norm_g=norm_g, sgu_g=sgu_g, qg=q_norm_g, kg=k_norm_g,
            wsp=wsp, wmask=hc['wmask'], bsp=bsp,
            augk=np.ascontiguousarray(augk), augq=np.ascontiguousarray(augq),
            maskb=hc['maskb'], ident=hc['ident'], sel=hc['sel'],
            xs=np.ascontiguousarray(x_sample[sl, 0, :]),
            ck=np.ascontiguousarray(cks[:, sl]), cv=np.ascontiguousarray(cvs[:, sl]),
            sbias=hc['sbias'], bdm=hc['bdm'], selq=hc['selq'], selo=hc['selo'], w00=w00, b00=b00,
        ))
    return in_maps


_NC_CACHE = {}


def kernel(x_prompt, x_sample, cache_k, cache_v, norm_g, w_in, sgu_g, w_spatial, b_spatial,
           q_norm_g, k_norm_g, w_out):
    f = lambda a: np.ascontiguousarray(np.asarray(a), dtype=np.float32)
    x_prompt, x_sample, cache_k, cache_v = f(x_prompt), f(x_sample), f(cache_k), f(cache_v)
    norm_g, w_in, sgu_g, w_spatial, b_spatial = f(norm_g), f(w_in), f(sgu_g), f(w_spatial), f(b_spatial)
    q_norm_g, k_norm_g, w_out = f(q_norm_g), f(k_norm_g), f(w_out)
    if 'nc' not in _NC_CACHE:
        _NC_CACHE['nc'] = build_nc()
    nc = _NC_CACHE['nc']
    in_maps = _prep(x_prompt, x_sample, cache_k, cache_v, norm_g, w_in, sgu_g, w_spatial, b_spatial,
                    q_norm_g, k_norm_g, w_out)
    res = run_bass_kernel_spmd(nc, in_maps, core_ids=list(range(NCORES)))
    R = res.results
    loc = TOK_OF_SLOT[:ST]
    y = np.zeros((1, 16384, D), np.float32)
    for c in range(NCORES):
        yc = np.asarray(R[c]["y"], dtype=np.float32)
        y[0, ST * c + loc] = yc
    ys = np.concatenate([np.asarray(R[c]["ys"], np.float32) for c in range(NCORES)], axis=0).reshape(32, 1, D)
    nk = np.zeros((2, 1, ST, H, DH), np.float32)
    nv = np.zeros((2, 1, ST, H, DH), np.float32)
    nkc = np.asarray(R[NCORES - 1]["nk"], np.float32)
    nvc = np.asarray(R[NCORES - 1]["nv"], np.float32)
    nk[:, 0, loc] = nkc.reshape(2, ST, H, DH)
    nv[:, 0, loc] = nvc.reshape(2, ST, H, DH)
    nks = np.concatenate([np.asarray(R[c]["nks"], np.float32) for c in range(NCORES)], axis=1).reshape(2, 32, 1, H, DH)
    nvs = np.concatenate([np.asarray(R[c]["nvs"], np.float32) for c in range(NCORES)], axis=1).reshape(2, 32, 1, H, DH)
    nsg = np.concatenate([np.asarray(R[c]["nsg"], np.float32) for c in range(NCORES)], axis=1).reshape(2, 32, 1, 256)
    return (y, ys, nk, nv, nks, nvs, nsg)
```
